# Optimizing a Trainium2 kernel written in Bass

```python
import jax, jax.numpy as jnp
from jax import lax
import numpy as np

D_MODEL = 1024
BATCH = 16
SEQ = 4096
DEPTH = 1
DEC_BATCH = 8
DEC_SEQ = 64
PAST_LEN = 4096

CHUNK = 64
N_META = 16
SB_HEADS = 8
SB_HEAD_DIM = 64
SB_WIDTH = SB_HEADS * SB_HEAD_DIM
SB_BLOCK = 128
RET_HEADS = 4
RET_DK = 128
RET_DV = 128
RET_WIDTH = RET_HEADS * RET_DV
MIX_WIDTH = SB_WIDTH + RET_WIDTH
IN_WIDTH = 3 * SB_WIDTH + 2 * RET_HEADS * RET_DK + 2 * RET_WIDTH
D_FF = 2816
CONV_W = 3
ROPE_BASE = 10000.0
EPS = 1e-5

kernel_name = "stickbreak_retention_hymba_convffn_stream"


def _rmsnorm(x, g):
    xf = x.astype(jnp.float32)
    y = xf * lax.rsqrt(jnp.mean(xf * xf, axis=-1, keepdims=True) + EPS)
    return (y * g.astype(jnp.float32)).astype(x.dtype)


def _heads(t, n):
    b, l, _ = t.shape
    return t.reshape(b, l, n, -1).transpose(0, 2, 1, 3)


def _project(x, g_norm_mix, w_in):
    h = _rmsnorm(x, g_norm_mix)
    p = h @ w_in
    s1 = SB_WIDTH
    s2 = 2 * SB_WIDTH
    s3 = 3 * SB_WIDTH
    s4 = s3 + RET_HEADS * RET_DK
    s5 = s4 + RET_HEADS * RET_DK
    s6 = s5 + RET_WIDTH
    sb_q, sb_k, sb_v, r_q, r_k, r_v, r_g = jnp.split(p, [s1, s2, s3, s4, s5, s6], axis=-1)
    return (_heads(sb_q, SB_HEADS), _heads(sb_k, SB_HEADS), _heads(sb_v, SB_HEADS),
            _heads(r_q, RET_HEADS), _heads(r_k, RET_HEADS), _heads(r_v, RET_HEADS), r_g)


def _stick_breaking(q, k, v, q_pos, k_pos):
    z = jnp.einsum('bhqd,bhkd->bhqk', q, k).astype(jnp.float32) * (SB_HEAD_DIM ** -0.5)
    mask = k_pos[None, :] < q_pos[:, None]
    log_1m = jnp.where(mask, jax.nn.log_sigmoid(-z), 0.0)
    later = lax.cumsum(log_1m, axis=z.ndim - 1, reverse=True) - log_1m
    a = jnp.where(mask, jnp.exp(jax.nn.log_sigmoid(z) + later), 0.0)
    return jnp.einsum('bhqk,bhkd->bhqd', a.astype(v.dtype), v)


def _stick_breaking_prompt(q, k, v):
    b, h, l, d = q.shape
    nb = -(-l // SB_BLOCK)
    lp = nb * SB_BLOCK
    pad = ((0, 0), (0, 0), (0, lp - l), (0, 0))
    qp, kp, vp = jnp.pad(q, pad), jnp.pad(k, pad), jnp.pad(v, pad)
    k_pos = jnp.arange(lp)
    q_blocks = qp.reshape(b, h, nb, SB_BLOCK, d).transpose(2, 0, 1, 3, 4)
    pos_blocks = k_pos.reshape(nb, SB_BLOCK)
    out = lax.map(lambda a: _stick_breaking(a[0], kp, vp, a[1], k_pos), (q_blocks, pos_blocks))
    return out.transpose(1, 2, 0, 3, 4).reshape(b, h, lp, d)[:, :, :l]


def _ret_log_gamma():
    return jnp.log(1.0 - 2.0 ** (-5.0 - jnp.arange(RET_HEADS, dtype=jnp.float32)))


def _rope(t, pos):
    half = t.shape[-1] // 2
    inv = ROPE_BASE ** (-jnp.arange(half, dtype=jnp.float32) / half)
    ang = pos.astype(jnp.float32)[:, None] * inv[None, :]
    cos, sin = jnp.cos(ang), jnp.sin(ang)
    t1, t2 = t[..., :half], t[..., half:]
    return jnp.concatenate([t1 * cos - t2 * sin, t1 * sin + t2 * cos], axis=-1)


def _retention_prep(r_q, r_k, r_v, pos):
    q = _rope(r_q.astype(jnp.float32), pos)
    k = _rope(r_k.astype(jnp.float32), pos) * (RET_DK ** -0.5)
    return q, k, r_v.astype(jnp.float32)


def _retention_chunk(s, q, k, v):
    l = q.shape[2]
    log_g = _ret_log_gamma()[:, None, None]
    n = jnp.arange(l, dtype=jnp.float32)
    diff = n[:, None] - n[None, :]
    decay = jnp.where(diff >= 0, jnp.exp(jnp.maximum(diff, 0.0) * log_g), 0.0)
    inner = jnp.einsum('bhqk,bhke->bhqe', jnp.einsum('bhqd,bhkd->bhqk', q, k) * decay, v)
    cross = jnp.einsum('bhqd,bhde->bhqe', q, s) * jnp.exp((n + 1.0)[:, None] * log_g)
    k_dec = k * jnp.exp((l - 1.0 - n)[:, None] * log_g)
    s_new = jnp.exp(l * log_g) * s + jnp.einsum('bhkd,bhke->bhde', k_dec, v)
    return s_new, inner + cross


def _head_norm(o, g, n_heads):
    b, _, l, _ = o.shape
    of = o.astype(jnp.float32)
    y = of * lax.rsqrt(jnp.mean(of * of, axis=-1, keepdims=True) + EPS)
    y = y * g.astype(jnp.float32).reshape(n_heads, 1, -1)
    return y.transpose(0, 2, 1, 3).reshape(b, l, -1)


def _merge(sb_o, ret_o, r_g, g_sb_out, g_ret_out, w_out, dtype):
    sb = _head_norm(sb_o, g_sb_out, SB_HEADS)
    ret = _head_norm(ret_o, g_ret_out, RET_HEADS) * jax.nn.silu(r_g.astype(jnp.float32))
    return jnp.concatenate([sb, ret], axis=-1).astype(dtype) @ w_out


def _conv_ffn(x, conv_prev, g_norm_ffn, w_up, conv_w, conv_b, w_down):
    h = _rmsnorm(x, g_norm_ffn)
    u = h @ w_up
    l = u.shape[1]
    ext = jnp.concatenate([conv_prev.astype(u.dtype), u], axis=1)
    c = conv_b
    for i in range(CONV_W):
        c = c + conv_w[i] * ext[:, i:i + l]
    gate, val = jnp.split(c, 2, axis=-1)
    return (jax.nn.silu(gate) * val) @ w_down, ext[:, -(CONV_W - 1):]


def _prompt_path(x, meta_tokens, g_norm_mix, w_in, g_sb_out, g_ret_out, w_out,
                 g_norm_ffn, w_up, conv_w, conv_b, w_down, g_norm_final):
    b, seq, d = x.shape
    meta = jnp.broadcast_to(meta_tokens.astype(x.dtype)[None], (b, N_META, d))
    xs = jnp.concatenate([meta, x], axis=1)
    l = N_META + seq
    sb_q, sb_k, sb_v, r_q, r_k, r_v, r_g = _project(xs, g_norm_mix, w_in)
    sb_o = _stick_breaking_prompt(sb_q, sb_k, sb_v)
    pos = jnp.arange(l) - N_META
    q, k, v = _retention_prep(r_q, r_k, r_v, pos)
    s0 = jnp.zeros((b, RET_HEADS, RET_DK, RET_DV), jnp.float32)
    s_meta, o_meta = _retention_chunk(s0, q[:, :, :N_META], k[:, :, :N_META], v[:, :, :N_META])
    nc = seq // CHUNK

    def to_chunks(t):
        return t[:, :, N_META:].reshape(b, RET_HEADS, nc, CHUNK, -1).transpose(2, 0, 1, 3, 4)

    def step(s, qkv):
        return _retention_chunk(s, qkv[0], qkv[1], qkv[2])

    s_final, o_chunks = lax.scan(step, s_meta, (to_chunks(q), to_chunks(k), to_chunks(v)))
    o_frames = o_chunks.transpose(1, 2, 0, 3, 4).reshape(b, RET_HEADS, seq, RET_DV)
    ret_o = jnp.concatenate([o_meta, o_frames], axis=2)
    h = xs + _merge(sb_o, ret_o, r_g, g_sb_out, g_ret_out, w_out, xs.dtype)
    conv0 = jnp.zeros((b, CONV_W - 1, 2 * D_FF), h.dtype)
    f, conv_state = _conv_ffn(h, conv0, g_norm_ffn, w_up, conv_w, conv_b, w_down)
    h = h + f
    y = _rmsnorm(h, g_norm_final)[:, N_META:]
    return y, sb_k, sb_v, s_final, conv_state


def _sample_path(x, cache_sb_k, cache_sb_v, state_ret, state_conv, g_norm_mix, w_in, g_sb_out,
                 g_ret_out, w_out, g_norm_ffn, w_up, conv_w, conv_b, w_down, g_norm_final):
    ls = x.shape[1]
    past = cache_sb_k.shape[2]
    sb_q, sb_k, sb_v, r_q, r_k, r_v, r_g = _project(x, g_norm_mix, w_in)
    k_all = jnp.concatenate([cache_sb_k.astype(sb_k.dtype), sb_k], axis=2)
    v_all = jnp.concatenate([cache_sb_v.astype(sb_v.dtype), sb_v], axis=2)
    sb_o = _stick_breaking(sb_q, k_all, v_all, past + jnp.arange(ls), jnp.arange(past + ls))
    pos = past + jnp.arange(ls)
    q, k, v = _retention_prep(r_q, r_k, r_v, pos)
    s_new, ret_o = _retention_chunk(state_ret.astype(jnp.float32), q, k, v)
    h = x + _merge(sb_o, ret_o, r_g, g_sb_out, g_ret_out, w_out, x.dtype)
    f, conv_state = _conv_ffn(h, state_conv, g_norm_ffn, w_up, conv_w, conv_b, w_down)
    h = h + f
    y = _rmsnorm(h, g_norm_final)
    return y, sb_k, sb_v, s_new, conv_state


def setup_inputs(seed: int = 0) -> dict:
    key = jax.random.key(seed)
    ks = jax.random.split(key, 18)
    f32 = jnp.float32
    nrm = lambda k, shape, s: jax.random.normal(k, shape, f32) * s
    return {
        "x_prompt": nrm(ks[0], (BATCH, SEQ, D_MODEL), 1.0),
        "x_sample": nrm(ks[1], (DEC_BATCH, DEC_SEQ, D_MODEL), 1.0),
        "cache_sb_k": nrm(ks[2], (DEC_BATCH, SB_HEADS, PAST_LEN, SB_HEAD_DIM), 1.0),
        "cache_sb_v": nrm(ks[3], (DEC_BATCH, SB_HEADS, PAST_LEN, SB_HEAD_DIM), 1.0),
        "state_ret": nrm(ks[4], (DEC_BATCH, RET_HEADS, RET_DK, RET_DV), 0.05),
        "state_conv": nrm(ks[5], (DEC_BATCH, CONV_W - 1, 2 * D_FF), 1.0),
        "meta_tokens": nrm(ks[6], (N_META, D_MODEL), 1.0),
        "g_norm_mix": 1.0 + nrm(ks[7], (D_MODEL,), 0.01),
        "w_in": nrm(ks[8], (D_MODEL, IN_WIDTH), D_MODEL ** -0.5),
        "g_sb_out": 1.0 + nrm(ks[9], (SB_WIDTH,), 0.01),
        "g_ret_out": 1.0 + nrm(ks[10], (RET_WIDTH,), 0.01),
        "w_out": nrm(ks[11], (MIX_WIDTH, D_MODEL), MIX_WIDTH ** -0.5),
        "g_norm_ffn": 1.0 + nrm(ks[12], (D_MODEL,), 0.01),
        "w_up": nrm(ks[13], (D_MODEL, 2 * D_FF), D_MODEL ** -0.5),
        "conv_w": nrm(ks[14], (CONV_W, 2 * D_FF), CONV_W ** -0.5),
        "conv_b": nrm(ks[15], (2 * D_FF,), 0.01),
        "w_down": nrm(ks[16], (D_FF, D_MODEL), D_FF ** -0.5),
        "g_norm_final": 1.0 + nrm(ks[17], (D_MODEL,), 0.01),
    }


def reference(x_prompt, x_sample, cache_sb_k, cache_sb_v, state_ret, state_conv, meta_tokens,
              g_norm_mix, w_in, g_sb_out, g_ret_out, w_out, g_norm_ffn, w_up, conv_w, conv_b,
              w_down, g_norm_final):
    y_prompt, sbk_p, sbv_p, ret_p, conv_p = _prompt_path(
        x_prompt, meta_tokens, g_norm_mix, w_in, g_sb_out, g_ret_out, w_out,
        g_norm_ffn, w_up, conv_w, conv_b, w_down, g_norm_final)
    y_sample, sbk_s, sbv_s, ret_s, conv_s = _sample_path(
        x_sample, cache_sb_k, cache_sb_v, state_ret, state_conv, g_norm_mix, w_in, g_sb_out,
        g_ret_out, w_out, g_norm_ffn, w_up, conv_w, conv_b, w_down, g_norm_final)
    return (y_prompt, y_sample, sbk_p, sbv_p, ret_p, conv_p, sbk_s, sbv_s, ret_s, conv_s)
```

```python
import numpy as np
import concourse.bass as bass
import concourse.mybir as mybir
from concourse.bass_utils import run_bass_kernel_spmd

F32 = mybir.dt.float32
BF16 = mybir.dt.bfloat16
AF = mybir.ActivationFunctionType
ALU = mybir.AluOpType

D = 1024
NMETA = 16
SEQ = 4096
LP = NMETA + SEQ
DEC = 64
PAST = 4096
INW = 3584
DFF = 2816
NFC = 44
EPS = 1e-5
NCORES = 8
LTOT = 2 * LP + DEC
SEQ_OFF = (0, LP, 2 * LP)
TB = 512

COMPUTE = ("pe", "act", "dve", "pool")
SEM_LIMIT = 30000


class Res:
    __slots__ = ("name", "last_w", "readers", "sems", "dcount")

    def __init__(self, name):
        self.name = name
        self.last_w = None
        self.readers = []
        self.sems = None
        self.dcount = 0


class Op:
    __slots__ = ("idx", "eng", "fn", "deps", "dma_res", "needs_inc", "ticket", "grp")

    def __init__(self, idx, eng, fn, dma_res):
        self.idx = idx
        self.eng = eng
        self.fn = fn
        self.deps = {}
        self.dma_res = dma_res
        self.needs_inc = False
        self.ticket = None
        self.grp = None


class Sched:
    def __init__(self, nc):
        self.nc = nc
        self.ops = []
        self.n_res = 0
        self.last_of = {}
        self.dma_since_bar = []

    def res(self, name=None):
        self.n_res += 1
        return Res(name or f"r{self.n_res}")

    def op(self, eng, fn, reads=(), writes=(), dma_res=None, grp=None):
        idx = len(self.ops)
        o = Op(idx, eng, fn, dma_res)
        o.grp = grp
        for r in reads:
            if r.last_w is not None:
                o.deps[r.last_w] = "RAW"
            r.readers.append(idx)
        for w in writes:
            if w.last_w is not None:
                o.deps.setdefault(w.last_w, "WAW")
            for rd in w.readers:
                if rd != idx:
                    o.deps.setdefault(rd, "WAR")
            w.last_w = idx
            w.readers = []
        keep = {}
        for d, kind in o.deps.items():
            dop = self.ops[d]
            if dop.dma_res is None and dma_res is None and dop.eng == eng:
                if eng == "pe":
                    continue
            keep[d] = kind
        o.deps = keep
        for d in keep:
            self.ops[d].needs_inc = True
        self.ops.append(o)
        if dma_res is None:
            self.last_of[eng] = idx
        else:
            self.dma_since_bar.append(idx)
        return o

    def barrier(self):
        deps = set(self.last_of.values()) | set(self.dma_since_bar)
        self.dma_since_bar = []
        for eng in ("sp",) + COMPUTE:
            idx = len(self.ops)
            o = Op(idx, eng, None, None)
            for d in deps:
                dop = self.ops[d]
                if dop.dma_res is None and dop.eng == eng:
                    continue
                o.deps[d] = "BAR"
                dop.needs_inc = True
            self.ops.append(o)

    def emit(self, final_wait_res=()):
        nc = self.nc
        ops = self.ops
        sems = []

        def new_sem(name):
            h = nc.alloc_semaphore(name=name)
            sems.append(h)
            return h

        eng_sem = {e: [new_sem(f"s_{e}0")] for e in COMPUTE}
        eng_cnt = {e: 0 for e in COMPUTE}
        for o in ops:
            if o.dma_res is not None:
                r = o.dma_res
                if r.sems is None:
                    r.sems = [new_sem(f"d_{r.name}")]
                    r.dcount = 0
                if r.dcount + 16 > SEM_LIMIT:
                    r.sems.append(new_sem(f"d_{r.name}_{len(r.sems)}"))
                    r.dcount = 0
                r.dcount += 16
                o.ticket = (r.sems[-1], r.dcount)
            elif o.needs_inc and o.fn is not None:
                e = o.eng
                if eng_cnt[e] + 1 > SEM_LIMIT:
                    eng_sem[e].append(new_sem(f"s_{e}{len(eng_sem[e])}"))
                    eng_cnt[e] = 0
                eng_cnt[e] += 1
                o.ticket = (eng_sem[e][-1], eng_cnt[e])
        self.n_sems = len(sems)
        by_eng = {}
        for o in ops:
            by_eng.setdefault(o.eng, []).append(o)
        final = []
        for r in final_wait_res:
            if r.sems is not None:
                final.append((r.sems[-1], r.dcount))

        def run(eh, ename, is_last=False):
            known = {}
            q = by_eng.get(ename, [])
            for qi, o in enumerate(q):
                need = {}
                srcs = [o]
                if o.grp is not None and (qi == 0 or q[qi - 1].grp != o.grp):
                    qj = qi + 1
                    while qj < len(q) and q[qj].grp == o.grp:
                        srcs.append(q[qj])
                        qj += 1
                for so_ in srcs:
                    for d in so_.deps:
                        t = ops[d].ticket
                        if t is None:
                            continue
                        sem, val = t
                        k = id(sem)
                        if k not in need or need[k][1] < val:
                            need[k] = (sem, val)
                for k, (sem, val) in need.items():
                    if known.get(k, 0) >= val:
                        continue
                    known[k] = val
                    eh.wait_ge(sem, val)
                if o.fn is None:
                    continue
                ins = o.fn(eh)
                if o.ticket is not None:
                    sem, val = o.ticket
                    ins.then_inc(sem, 16 if o.dma_res is not None else 1)
            if is_last:
                for sem, val in final:
                    eh.wait_ge(sem, val)

        with nc.Block() as block:
            @block.sync
            def _(e):
                run(e, "sp", is_last=True)

            @block.tensor
            def _(e):
                run(e, "pe")

            @block.scalar
            def _(e):
                run(e, "act")

            @block.vector
            def _(e):
                run(e, "dve")

            @block.gpsimd
            def _(e):
                run(e, "pool")


class Buf:
    __slots__ = ("ap", "r")

    def __init__(self, ap, r):
        self.ap = ap
        self.r = r


class Arena:
    def __init__(self, S, tensor, ncols):
        self.S = S
        self.t = tensor
        self.ncols = ncols
        self.off = 0
        self.hi = 0

    def _take(self, name, cols_bf):
        cols_al = (cols_bf + 31) // 32 * 32
        assert self.off + cols_al <= self.ncols, (name, self.off, cols_al, self.ncols)
        o = self.off
        self.off += cols_al
        self.hi = max(self.hi, self.off)
        return o


class ArenaView:
    def __init__(self, arena, f32):
        self.a = arena
        self.f32 = f32

    @property
    def off(self):
        return self.a.off

    @off.setter
    def off(self, v):
        self.a.off = v

    def alloc(self, name, cols, parts=128):
        a = self.a
        if self.f32:
            o = a._take(name, 2 * cols)
            ap = a.t[0:parts, o:o + 2 * cols].bitcast(F32)
        else:
            o = a._take(name, cols)
            ap = a.t[0:parts, o:o + cols]
        return Buf(ap, a.S.res(name))


def _consts():
    c = {}
    c["c_ident"] = np.eye(128, dtype=np.float32)
    j = np.arange(128)
    c["c_tri"] = (j[:, None] >= j[None, :]).astype(np.float32)
    c["c_ones"] = np.ones((128, 128), np.float32)
    tp = np.arange(896)
    c["c_mask"] = (j[:, None] < (tp[None, :] - 384)).astype(np.float32)
    hh = np.arange(4, dtype=np.float64)
    gam = 1.0 - 2.0 ** (-5.0 - hh)
    logg = np.log(gam)
    diff = j[None, :] - j[:, None]
    dec = np.where(diff[None] >= 0, np.exp(np.maximum(diff[None], 0) * logg[:, None, None]), 0.0)
    c["c_dec"] = np.ascontiguousarray(dec.transpose(1, 0, 2)).astype(np.float32)
    cg = np.exp((j[None, :] + 1.0) * logg[:, None])
    c["c_crossg"] = np.ascontiguousarray(np.broadcast_to(cg[None], (128, 4, 128))).astype(np.float32)
    kd = np.zeros((128, 3, 4), np.float64)
    for li, l in enumerate((16, 64, 128)):
        for h in range(4):
            kd[:l, li, h] = np.exp((l - 1.0 - j[:l]) * logg[h])
    c["c_kd"] = np.repeat(kd.reshape(128, 12), 128, axis=1).astype(np.float32)
    pos = np.concatenate([np.arange(LP) - NMETA, PAST + np.arange(DEC)]).astype(np.float32)
    half = 64
    inv = (10000.0 ** (-np.arange(half, dtype=np.float32) / half)).astype(np.float32)
    ang = (pos[:, None] * inv[None, :]).astype(np.float32)
    cs, sn = np.cos(ang.astype(np.float64)), np.sin(ang.astype(np.float64))
    s = 128.0 ** -0.5
    t4 = lambda a: np.tile(a, (1, 4))
    c["c_rope"] = np.concatenate([t4(cs), t4(sn), t4(cs * s), t4(sn * s)], axis=1).astype(np.float32)
    gl = {l: [float(np.exp(l * logg[h])) for h in range(4)] for l in (16, 64, 128)}
    return c, gl


def build_program():
    consts, GL = _consts()
    import os
    STAGE = int(os.environ.get('KSTAGE', '9'))
    KSUB = float(os.environ.get('KSUB', '99'))
    KRET = int(os.environ.get('KRET', '7'))
    nc = bass.Bass("TRN2", target_bir_lowering=False)

    def din(name, shape):
        return nc.dram_tensor(name, list(shape), F32, kind="ExternalInput").ap()

    def dout(name, shape):
        return nc.dram_tensor(name, list(shape), F32, kind="ExternalOutput").ap()

    xp = din("xp", (2, SEQ, D))
    xs = din("xs", (DEC, D))
    ck = din("ck", (8, PAST, 64))
    cv = din("cv", (8, PAST, 64))
    sret = din("sret", (4, 128, 128))
    sconv = din("sconv", (2, 5632))
    meta = din("meta", (NMETA, D))
    g_mix = din("g_mix", (D,))
    w_in = din("w_in", (D, INW))
    g_sbo = din("g_sbo", (512,))
    g_reto = din("g_reto", (512,))
    w_out = din("w_out", (D, D))
    g_ffn = din("g_ffn", (D,))
    w_up = din("w_up", (D, 2 * DFF))
    conv_w = din("conv_w", (3, 2 * DFF))
    conv_b = din("conv_b", (2 * DFF,))
    w_down = din("w_down", (DFF, D))
    g_fin = din("g_fin", (D,))
    cdr = {k: din(k, v.shape) for k, v in consts.items()}

    y_p = dout("y_p", (2, SEQ, D))
    y_s = dout("y_s", (DEC, D))
    sbk_p = dout("sbk_p", (2, 8, LP, 64))
    sbv_p = dout("sbv_p", (2, 8, LP, 64))
    ret_p = dout("ret_p", (2, 4, 128, 128))
    conv_p = dout("conv_p", (2, 2, 5632))
    sbk_s = dout("sbk_s", (8, DEC, 64))
    sbv_s = dout("sbv_s", (8, DEC, 64))
    ret_s = dout("ret_s", (4, 128, 128))
    conv_s = dout("conv_s", (2, 5632))

    qt_scr = nc.dram_tensor("qt_scr", [4, 128, LTOT], BF16).ap()
    mrt_scr = nc.dram_tensor("mrt_scr", [4, 128, LTOT], BF16).ap()
    h2_scr = nc.dram_tensor("h2_scr", [LTOT, D], F32).ap()

    S = Sched(nc)
    out_res = []

    seqs = [("p", 0), ("p", 1), ("s", 0)]

    def seq_chunks(kind):
        if kind == "p":
            return [(0, NMETA)] + [(NMETA + 128 * c, 128) for c in range(SEQ // 128)]
        return [(0, DEC)]

    def x_rows(kind, si, n0, l):
        if kind == "p":
            if n0 == 0:
                return meta[0:l, :]
            return xp[si, n0 - NMETA:n0 - NMETA + l, :]
        return xs[n0:n0 + l, :]

    def rope_row0(kind, n0):
        return n0 if kind == "p" else LP + n0

    NARENA = 106368
    from contextlib import ExitStack
    with ExitStack() as es:
        arena_t = es.enter_context(nc.sbuf_tensor("arena", [128, NARENA], BF16))
        _arena = Arena(S, arena_t, NARENA)
        AB = ArenaView(_arena, False)
        AFa = ArenaView(_arena, True)
        pbank = []
        for i in range(7):
            t = es.enter_context(nc.psum_tensor(f"pb{i}", [128, 512], F32))
            pbank.append(Buf(t[:, :], S.res(f"pb{i}")))
        tpt = es.enter_context(nc.psum_tensor("tp", [128, 1024], BF16))
        TP = Buf(tpt[:, :], S.res("tp"))

        def dma(q, out_ap, in_ap, reads, writes, res, slow=False):
            if slow:
                fn = lambda e: e.dma_start(out=out_ap, in_=in_ap, allow_slow_non_contiguous=True)
            else:
                fn = lambda e: e.dma_start(out=out_ap, in_=in_ap)
            S.op(q, fn, reads=reads, writes=writes, dma_res=res)

        def mm(out_ap, lhsT, rhs, start, stop, reads, writes, grp=None):
            S.op("pe", lambda e: e.matmul(out_ap, lhsT=lhsT, rhs=rhs, start=start, stop=stop),
                 reads=reads, writes=writes, grp=grp)

        def tr(out_ap, in_ap, ident_ap, reads, writes):
            S.op("pe", lambda e: e.transpose(out_ap, in_ap, ident_ap), reads=reads, writes=writes)

        def act(out_ap, in_ap, func, reads, writes, bias=None, scale=None, accum=None):
            kw = {}
            if bias is not None:
                kw["bias"] = bias
            if scale is not None:
                kw["scale"] = scale
            if accum is not None:
                kw["accum_out"] = accum
            S.op("act", lambda e: e.activation(out=out_ap, in_=in_ap, func=func, **kw), reads=reads, writes=writes)

        def tt(eng, out_ap, in0, in1, op, reads, writes):
            S.op(eng, lambda e: e.tensor_tensor(out=out_ap, in0=in0, in1=in1, op=op), reads=reads, writes=writes)

        def ts1(eng, out_ap, in0, s1, s2, op0, op1, reads, writes):
            if s2 is None:
                fn = lambda e: e.tensor_scalar(out=out_ap, in0=in0, scalar1=s1, scalar2=None, op0=op0)
            else:
                fn = lambda e: e.tensor_scalar(out=out_ap, in0=in0, scalar1=s1, scalar2=s2, op0=op0, op1=op1)
            S.op(eng, fn, reads=reads, writes=writes)

        def stt(eng, out_ap, in0, scalar, in1, op0, op1, reads, writes):
            S.op(eng, lambda e: e.scalar_tensor_tensor(out=out_ap, in0=in0, scalar=scalar, in1=in1, op0=op0, op1=op1),
                 reads=reads, writes=writes)

        def cpy(eng, out_ap, in_ap, reads, writes):
            if eng == "act":
                S.op("act", lambda e: e.activation(out=out_ap, in_=in_ap, func=AF.Copy), reads=reads, writes=writes)
            else:
                S.op(eng, lambda e: e.tensor_copy(out=out_ap, in_=in_ap), reads=reads, writes=writes)

        def recip(out_ap, in_ap, reads, writes):
            S.op("dve", lambda e: e.reciprocal(out=out_ap, in_=in_ap), reads=reads, writes=writes)

        def mset(eng, ap, val, writes):
            S.op(eng, lambda e: e.memset(ap, val), writes=writes)

        def v3(ap, a):
            return ap.rearrange("p (a b) -> p a b", a=a)

        ident = AB.alloc("ident", 128)
        tri = AB.alloc("tri", 128)
        ones = AB.alloc("ones", 128)
        od128 = AB.alloc("od128", 128)
        od64 = AB.alloc("od64", 128)
        dma("pool", ident.ap, cdr["c_ident"], [], [ident.r], ident.r)
        dma("pool", tri.ap, cdr["c_tri"], [], [tri.r], tri.r)
        dma("pool", ones.ap, cdr["c_ones"], [], [ones.r], ones.r)
        ts1("pool", od128.ap, ones.ap, 1.0 / 128.0, None, ALU.mult, None, [ones.r], [od128.r])
        ts1("pool", od64.ap, ones.ap, 1.0 / 64.0, None, ALU.mult, None, [ones.r], [od64.r])
        AB_BASE = AB.off

        wstage = None

        def load_weight(dst, k_rows, src_rows_ap, ncols, gcol_ap, gres, piece=2048):
            c0 = 0
            i = 0
            while c0 < ncols:
                cw = min(piece, ncols - c0)
                st = wstage[load_weight.cnt % 2]
                load_weight.cnt += 1
                dma("sp", st.ap[0:k_rows, 0:cw], src_rows_ap[:, c0:c0 + cw], [], [st.r], st.r)
                on_act = (load_weight.cnt % 2 == 0)
                if gcol_ap is None:
                    cpy("act" if on_act else "dve", dst.ap[0:k_rows, c0:c0 + cw], st.ap[0:k_rows, 0:cw], [st.r], [dst.r])
                elif on_act:
                    act(dst.ap[0:k_rows, c0:c0 + cw], st.ap[0:k_rows, 0:cw], AF.Copy, [st.r, gres], [dst.r], scale=gcol_ap)
                else:
                    ts1("dve", dst.ap[0:k_rows, c0:c0 + cw], st.ap[0:k_rows, 0:cw],
                        gcol_ap, None, ALU.mult, None, [st.r, gres], [dst.r])
                c0 += cw
                i += 1
        load_weight.cnt = 0

        def rms_rstd(xt, l, junk, stat):
            mset("pool", stat.ap[0:l, 0:1], 0.0, [stat.r])
            act(junk.ap[0:l, 0:D], xt.ap[0:l, 0:D], AF.Square, [xt.r, stat.r], [junk.r, stat.r],
                scale=1.0 / 32.0, accum=stat.ap[0:l, 0:1])
            act(stat.ap[0:l, 1:2], stat.ap[0:l, 0:1], AF.Sqrt, [stat.r], [stat.r], bias=EPS)
            recip(stat.ap[0:l, 1:2], stat.ap[0:l, 1:2], [stat.r], [stat.r])

        def transposes_to(dst3, src, l, nblk, dst_reads_extra=()):
            tp3 = v3(TP.ap[:, 0:nblk * 128], nblk)
            for k in range(nblk):
                tr(tp3[:, k, 0:l], src.ap[0:l, k * 128:(k + 1) * 128], ident.ap[0:l, 0:l],
                   [src.r, ident.r], [TP.r])
            return tp3

        w_in_bf = AB.alloc("w_in_bf", 8 * INW)
        gcol = AFa.alloc("gcol", 8)
        wstage = [AFa.alloc("wst0", 2048), AFa.alloc("wst1", 2048)]
        dma("sp", gcol.ap, g_mix.rearrange("(c p) -> p c", p=128), [], [gcol.r], gcol.r, slow=True)
        w_in3 = v3(w_in_bf.ap, 8)
        w_in_res = [S.res(f"w_in_cg{cg}") for cg in range(7)]
        for cg in range(7 if STAGE >= 1 else 0):
            for k in range(8):
                load_weight(Buf(w_in3[:, k, cg * 512:(cg + 1) * 512], w_in_res[cg]), 128,
                            w_in[k * 128:(k + 1) * 128, cg * 512:(cg + 1) * 512], 512,
                            gcol.ap[:, k:k + 1], gcol.r, piece=512)

        xt = [AFa.alloc("xt0", D), AFa.alloc("xt1", D)]
        junk = AFa.alloc("junk", D)
        stat = AFa.alloc("stat", 16)
        hn = AB.alloc("hn", D)
        hT = AB.alloc("hT", 8 * 128)
        q_bf = AB.alloc("q_bf", 512)
        qT_sb = [AB.alloc("qT_sb0", 512), AB.alloc("qT_sb1", 512)]
        kst = [AFa.alloc("kst0", 512), AFa.alloc("kst1", 512)]
        vst = [AFa.alloc("vst0", 512), AFa.alloc("vst1", 512)]
        rqk32 = AFa.alloc("rqk32", 1024)
        ropet = AFa.alloc("ropet", 1024)
        rtmp = [AFa.alloc("rtmpa", 256), AFa.alloc("rtmpb", 256)]
        rqk_bf2 = [AB.alloc("rqk_bf0", 1024), AB.alloc("rqk_bf1", 1024)]
        rv_bf2 = [AB.alloc("rv_bf0", 512), AB.alloc("rv_bf1", 512)]
        kdec_bf = AB.alloc("kdec_bf", 512)
        sg322 = [AFa.alloc("sg32_0", 512), AFa.alloc("sg32_1", 512)]
        rqT = AB.alloc("rqT", 512)
        rkT = AB.alloc("rkT", 512)
        rqgT = AB.alloc("rqgT", 512)
        PT = AB.alloc("PT", 512)
        S32 = AFa.alloc("S32", 512)
        S_bf = AB.alloc("S_bf", 512)
        decT = AFa.alloc("decT", 512)
        crossg = AFa.alloc("crossg", 512)
        kdtab = AFa.alloc("kdtab", 1536)
        ostat = AFa.alloc("ostat", 16)
        mret_bf = AB.alloc("mret_bf", 512)
        mrT_sb = [AB.alloc("mrT_sb0", 512), AB.alloc("mrT_sb1", 512)]
        dma("sp", decT.ap, cdr["c_dec"].rearrange("j h i -> j (h i)"), [], [decT.r], decT.r)
        dma("sp", crossg.ap, cdr["c_crossg"].rearrange("j h i -> j (h i)"), [], [crossg.r], crossg.r)
        dma("sp", kdtab.ap, cdr["c_kd"], [], [kdtab.r], kdtab.r)

        R1, R2, R3 = pbank[4], pbank[5], pbank[6]
        mmb = [pbank[0], pbank[1], pbank[2], pbank[3]]
        mmi = [0]

        def next_bank():
            b = mmb[mmi[0] % len(mmb)]
            mmi[0] += 1
            return b

        chunk_id = [0]
        deferred = []

        def a1_load(kind, si, n0, l, slot):
            dma("sp", xt[slot].ap[0:l, :], x_rows(kind, si, n0, l), [], [xt[slot].r], xt[slot].r)

        def a1_front(kind, si, sidx, n0, l, slot, x, par, goff, rqk_bf, rv_bf, sg32):
            rms_rstd(x, l, junk, stat)
            ts1("dve", hn.ap[0:l, :], x.ap[0:l, :], stat.ap[0:l, 1:2], None, ALU.mult, None, [x.r, stat.r], [hn.r])
            yield
            tp3 = transposes_to(None, hn, l, 8)
            hT3 = v3(hT.ap, 8)
            cpy("act", hT3[:, :, 0:l], tp3[:, :, 0:l], [TP.r], [hT.r])
            r0 = rope_row0(kind, n0)
            dma("sp", ropet.ap[0:l, :], cdr["c_rope"][r0:r0 + l, :], [], [ropet.r], ropet.r)
            yield

            if KSUB < 2:
                return

            def proj(cg):
                b = next_bank()
                for k in range(8):
                    mm(b.ap[0:l, :], hT3[:, k, 0:l], w_in3[:, k, cg * 512:(cg + 1) * 512], k == 0, k == 7,
                       [hT.r, w_in_res[cg]], [b.r])
                return b

            b = proj(0)
            cpy("act", q_bf.ap[0:l, :], b.ap[0:l, :], [b.r], [q_bf.r])
            yield
            tp3q = transposes_to(None, q_bf, l, 4)
            qs = qT_sb[par]
            cpy("dve", v3(qs.ap, 4)[:, :, 0:l], tp3q[:, :, 0:l], [TP.r], [qs.r])
            dma("sp", qt_scr[:, :, goff:goff + l].rearrange("a p n -> p a n"), v3(qs.ap, 4)[:, :, 0:l],
                [qs.r], [], qs.r)
            yield
            if KSUB < 3:
                return
            for cg, stg, outp, outs in ((1, kst[par], sbk_p, sbk_s), (2, vst[par], sbv_p, sbv_s)):
                b = proj(cg)
                cpy("act" if cg == 1 else "dve", stg.ap[0:l, :], b.ap[0:l, :], [b.r], [stg.r])
                if kind == "p":
                    dst = outp[si, :, n0:n0 + l, :].rearrange("h n d -> n h d")
                else:
                    dst = outs[:, n0:n0 + l, :].rearrange("h n d -> n h d")
                dma("sp", dst, stg.ap[0:l, :].rearrange("p (h d) -> p h d", h=8), [stg.r], [], stg.r)
                yield
            if KSUB < 4:
                return
            for cg in (3, 4):
                b = proj(cg)
                cpy("act", rqk32.ap[0:l, (cg - 3) * 512:(cg - 2) * 512], b.ap[0:l, :], [b.r], [rqk32.r])
                yield
            for qk in range(2):
                src = rqk32.ap[0:l, qk * 512:(qk + 1) * 512].rearrange("p (h t d) -> p h t d", h=4, t=2)
                dst = rqk_bf.ap[0:l, qk * 512:(qk + 1) * 512].rearrange("p (h t d) -> p h t d", h=4, t=2)
                cosb = v3(ropet.ap[0:l, qk * 512:qk * 512 + 256], 4)
                sinb = v3(ropet.ap[0:l, qk * 512 + 256:qk * 512 + 512], 4)
                ta = v3(rtmp[0].ap[0:l, :], 4)
                tb = v3(rtmp[1].ap[0:l, :], 4)
                t1, t2 = src[:, :, 0, :], src[:, :, 1, :]
                tt("dve", ta, t1, cosb, ALU.mult, [rqk32.r, ropet.r], [rtmp[0].r])
                tt("dve", tb, t2, sinb, ALU.mult, [rqk32.r, ropet.r], [rtmp[1].r])
                tt("dve", dst[:, :, 0, :], ta, tb, ALU.subtract, [rtmp[0].r, rtmp[1].r], [rqk_bf.r])
                yield
                tt("dve", ta, t1, sinb, ALU.mult, [rqk32.r, ropet.r], [rtmp[0].r])
                tt("dve", tb, t2, cosb, ALU.mult, [rqk32.r, ropet.r], [rtmp[1].r])
                tt("dve", dst[:, :, 1, :], ta, tb, ALU.add, [rtmp[0].r, rtmp[1].r], [rqk_bf.r])
                yield
            if KSUB < 5:
                return
            b = proj(5)
            cpy("act", rv_bf.ap[0:l, :], b.ap[0:l, :], [b.r], [rv_bf.r])
            yield
            b = proj(6)
            act(sg32.ap[0:l, :], b.ap[0:l, :], AF.Silu, [b.r], [sg32.r])

        def a1_chunk(kind, si, sidx, n0, l, slot, first, last, phase):
            li = {16: 0, 64: 1, 128: 2}[l]
            par = slot
            goff = SEQ_OFF[sidx] + n0
            x = xt[slot]
            rqk_bf = rqk_bf2[par]
            rv_bf = rv_bf2[par]
            sg32 = sg322[par]
            if phase == 0:
                yield from a1_front(kind, si, sidx, n0, l, slot, x, par, goff, rqk_bf, rv_bf, sg32)
                return
            hT3 = v3(hT.ap, 8)
            if KSUB < 6:
                return
            kd = kdtab.ap[0:l, li * 512:(li + 1) * 512]
            if KSUB >= 6.02:
                tt("dve", kdec_bf.ap[0:l, :], rqk_bf.ap[0:l, 512:1024], kd, ALU.mult,
                   [rqk_bf.r, kdtab.r], [kdec_bf.r])
            yield
            tp3r = transposes_to(None, rqk_bf, l, 8)
            cpy("act", v3(rqT.ap, 4)[:, :, 0:l], tp3r[:, 0:4, 0:l], [TP.r], [rqT.r])
            cpy("act", v3(rkT.ap, 4)[:, :, 0:l], tp3r[:, 4:8, 0:l], [TP.r], [rkT.r])
            if KSUB >= 6.05:
                tt("dve", v3(rqgT.ap, 4)[:, :, 0:l], v3(rqT.ap, 4)[:, :, 0:l], v3(crossg.ap, 4)[:, :, 0:l], ALU.mult,
                   [rqT.r, crossg.r], [rqgT.r])
            yield
            if KSUB < 6.2:
                return
            if first:
                if kind == "p":
                    mset("pool", S32.ap, 0.0, [S32.r])
                else:
                    dma("sp", v3(S32.ap, 4), sret.rearrange("h k v -> k h v"), [], [S32.r], S32.r)
                cpy("act", S_bf.ap, S32.ap, [S32.r], [S_bf.r])
                yield
            if KSUB < 6.4:
                return
            r1 = v3(R1.ap, 4)
            for h in range(4):
                mm(r1[0:l, h, 0:l], v3(rkT.ap, 4)[:, h, 0:l], v3(rqT.ap, 4)[:, h, 0:l], True, True,
                   [rkT.r, rqT.r], [R1.r])
            tt("dve", v3(PT.ap, 4)[0:l, :, 0:l], r1[0:l, :, 0:l], v3(decT.ap, 4)[0:l, :, 0:l], ALU.mult,
               [R1.r, decT.r], [PT.r])
            yield
            r2 = v3(R2.ap, 4)
            r3 = v3(R3.ap, 4)
            if KSUB < 6.6:
                return
            for h in range(4):
                mm(r2[0:l, h, :], v3(PT.ap, 4)[0:l, h, 0:l], rv_bf.ap[0:l, h * 128:(h + 1) * 128], True, False,
                   [PT.r, rv_bf.r], [R2.r])
                mm(r2[0:l, h, :], v3(rqgT.ap, 4)[:, h, 0:l], v3(S_bf.ap, 4)[:, h, :], False, True,
                   [rqgT.r, S_bf.r], [R2.r])
            yield
            if KSUB < 6.8:
                return
            for h in range(4):
                mm(r3[:, h, :], v3(kdec_bf.ap, 4)[0:l, h, :], rv_bf.ap[0:l, h * 128:(h + 1) * 128], True, True,
                   [kdec_bf.r, rv_bf.r], [R3.r])
            yield
            for h in range(4):
                ts1("dve", v3(S32.ap, 4)[:, h, :], v3(S32.ap, 4)[:, h, :], GL[l][h], None, ALU.mult, None,
                    [S32.r], [S32.r])
                tt("dve", v3(S32.ap, 4)[:, h, :], v3(S32.ap, 4)[:, h, :], r3[:, h, :], ALU.add,
                   [S32.r, R3.r], [S32.r])
                if h % 2 == 1:
                    yield
            cpy("act", S_bf.ap, S32.ap, [S32.r], [S_bf.r])
            if last and KSUB >= 6.95 and (KRET >> sidx) & 1:
                if os.environ.get("KDUMMY"):
                    stg = kst[par]
                    dstk = (sbk_p[si, :, n0:n0 + l, :] if kind == "p" else sbk_s[:, n0:n0 + l, :]).rearrange("h n d -> n h d")
                    dma("sp", dstk, stg.ap[0:l, :].rearrange("p (h d) -> p h d", h=8), [stg.r], [], stg.r)
                else:
                    for h in range(4):
                        dsth = ret_p[si, h] if kind == "p" else ret_s[h]
                        dma("sp", dsth, v3(S32.ap, 4)[:, h, :], [S32.r], [], S32.r)
            yield
            if KSUB < 7:
                return
            mset("pool", ostat.ap[0:l, 0:4], 0.0, [ostat.r])
            for h in range(4):
                act(junk.ap[0:l, h * 128:(h + 1) * 128], r2[0:l, h, :], AF.Square, [R2.r, ostat.r], [junk.r, ostat.r],
                    scale=float(128.0 ** -0.5), accum=ostat.ap[0:l, h:h + 1])
            act(ostat.ap[0:l, 4:8], ostat.ap[0:l, 0:4], AF.Sqrt, [ostat.r], [ostat.r], bias=EPS)
            recip(ostat.ap[0:l, 4:8], ostat.ap[0:l, 4:8], [ostat.r], [ostat.r])
            yield
            for h in range(4):
                stt("dve", mret_bf.ap[0:l, h * 128:(h + 1) * 128], r2[0:l, h, :], ostat.ap[0:l, 4 + h:5 + h],
                    sg32.ap[0:l, h * 128:(h + 1) * 128], ALU.mult, ALU.mult, [R2.r, ostat.r, sg32.r], [mret_bf.r])
            yield
            tp3m = transposes_to(None, mret_bf, l, 4)
            ms = mrT_sb[par]
            cpy("act", v3(ms.ap, 4)[:, :, 0:l], tp3m[:, :, 0:l], [TP.r], [ms.r])
            dma("sp", mrt_scr[:, :, goff:goff + l].rearrange("a p n -> p a n"), v3(ms.ap, 4)[:, :, 0:l],
                [ms.r], [], ms.r)

        work = []
        for sidx, (kind, si) in enumerate(seqs):
            ch = seq_chunks(kind)
            for ci, (n0, l) in enumerate(ch):
                work.append((kind, si, sidx, n0, l, ci == 0, ci == len(ch) - 1))
        if STAGE < 2:
            work = []
        else:
            a1_load(work[0][0], work[0][1], work[0][3], work[0][4], 0)
        def run_rr(gens):
            gens = list(gens)
            while gens:
                for g in list(gens):
                    try:
                        next(g)
                    except StopIteration:
                        gens.remove(g)

        for wi in range(len(work) + 1):
            gens = []
            if wi < len(work):
                kind, si, sidx, n0, l, first, last = work[wi]
                if wi + 1 < len(work):
                    nk_ = work[wi + 1]
                    a1_load(nk_[0], nk_[1], nk_[3], nk_[4], (wi + 1) % 2)
                gens.append(a1_chunk(kind, si, sidx, n0, l, wi % 2, first, last, 0))
            if wi >= 1:
                kind, si, sidx, n0, l, first, last = work[wi - 1]
                gens.append(a1_chunk(kind, si, sidx, n0, l, (wi - 1) % 2, first, last, 1))
            run_rr(gens)
        out_res.extend([kst[0].r, kst[1].r, vst[0].r, vst[1].r, S32.r])

        S.barrier()
        for dsth, srch in deferred:
            dma("sp", dsth, srch, [S32.r], [], S32.r)
        if deferred:
            S.barrier()
        AB.off = AB_BASE
        wo_sb = AB.alloc("wo_sb", 8 * D, parts=64)
        wo_ret = AB.alloc("wo_ret", 4 * D)
        KT = AB.alloc("KT", 4 * 4224)
        Vt = AB.alloc("Vt", 34 * 512)
        gcs = AFa.alloc("gcs", 8, parts=64)
        gcr = AFa.alloc("gcr", 8)
        wstage = [AFa.alloc("wst0b", 1024), AFa.alloc("wst1b", 1024)]
        dma("sp", gcs.ap, g_sbo.rearrange("(h p) -> p h", p=64), [], [gcs.r], gcs.r, slow=True)
        dma("sp", gcr.ap[:, 0:4], g_reto.rearrange("(h p) -> p h", p=128), [], [gcr.r], gcr.r, slow=True)
        wos3 = v3(wo_sb.ap, 8)
        wor3 = v3(wo_ret.ap, 4)
        for h in range(8):
            load_weight(Buf(wos3[:, h, :], wo_sb.r), 64, w_out[h * 64:(h + 1) * 64, :], D, gcs.ap[:, h:h + 1], gcs.r,
                        piece=1024)
        for h in range(4):
            load_weight(Buf(wor3[:, h, :], wo_ret.r), 128, w_out[512 + h * 128:512 + (h + 1) * 128, :], D,
                        gcr.ap[:, h:h + 1], gcr.r, piece=1024)
        maskb = AB.alloc("maskb", 896)
        dma("pool", maskb.ap, cdr["c_mask"], [], [maskb.r], maskb.r)
        kblk = [AB.alloc("kblk0", 512), AB.alloc("kblk1", 512)]
        QTs = [AB.alloc("QTs0", 4 * TB), AB.alloc("QTs1", 4 * TB)]
        mrTs2 = [AB.alloc("mrTs0", 4 * TB), AB.alloc("mrTs1", 4 * TB)]
        msbT = AB.alloc("msbT", 8 * TB, parts=64)
        NSL = 4
        NEL = 6
        NFILL = int(os.environ.get('KFILL', '2'))
        e_sb = [AFa.alloc(f"e_sb{i}", TB) for i in range(NEL)]
        g_sb = [AFa.alloc(f"g_sb{i}", TB) for i in range(3)]
        sp_sb = [AB.alloc(f"sp_sb{i}", TB) for i in range(NSL)]
        srun = [AB.alloc(f"srun{i}", TB) for i in range(2)]
        a_sb = [AB.alloc(f"a_sb{i}", TB) for i in range(3)]
        OTs = [AFa.alloc(f"OTs{i}", TB, parts=64) for i in range(2)]
        sq_sb = AB.alloc("sq_sb", TB, parts=64)
        rt_sb = AFa.alloc("rt_sb", TB, parts=64)
        xt2 = [AFa.alloc("xt2_0", D), AFa.alloc("xt2_1", D)]
        h2s = [AFa.alloc("h2s_0", D), AFa.alloc("h2s_1", D)]
        ZB = [pbank[0], pbank[1]]
        CB = [pbank[2], pbank[3]]
        OB = [pbank[4], pbank[5]]
        MSB = pbank[6]
        KT3 = v3(KT.ap, 4)
        Vt3 = v3(Vt.ap, 34)
        ucnt = [0]

        def a2_seq(kind, si, sidx):
            blocks = []
            if kind == "p":
                blocks.append((0, NMETA))
                for c in range(SEQ // 128):
                    blocks.append((NMETA + 128 * c, 128))
                ksrc = lambda n0, nk: sbk_p[si, :, n0:n0 + nk, :].rearrange("h n d -> n h d")
                vsrc = lambda n0, nk: sbv_p[si, :, n0:n0 + nk, :].rearrange("h n d -> n h d")
            else:
                for c in range(PAST // 128):
                    blocks.append((128 * c, 128))
                blocks.append((PAST, DEC))

                def ksrc(n0, nk):
                    if n0 < PAST:
                        return ck[:, n0:n0 + nk, :].rearrange("h n d -> n h d")
                    return sbk_s[:, n0 - PAST:n0 - PAST + nk, :].rearrange("h n d -> n h d")

                def vsrc(n0, nk):
                    if n0 < PAST:
                        return cv[:, n0:n0 + nk, :].rearrange("h n d -> n h d")
                    return sbv_s[:, n0 - PAST:n0 - PAST + nk, :].rearrange("h n d -> n h d")
            for bi, (n0, nk) in enumerate(blocks):
                kb = kblk[bi % 2]
                dma("pool", kb.ap[0:nk, :].rearrange("p (h d) -> p h d", h=8), ksrc(n0, nk), [], [kb.r], kb.r)
                tp3 = transposes_to(None, kb, nk, 4)
                cpy("act" if bi % 2 else "dve", KT3[:, :, n0:n0 + nk], tp3[:, :, 0:nk], [TP.r], [KT.r])
                dma("pool", Vt3[0:nk, bi, :].rearrange("p (h d) -> p h d", h=8), vsrc(n0, nk), [], [Vt.r], Vt.r)

            tiles = []
            if kind == "p":
                tiles.append((0, NMETA, [(0, 0)]))
                for i in range(SEQ // TB):
                    nb = TB // 128
                    kl = [(1 + nb * i + r, r) for r in range(nb - 1, -1, -1)]
                    kl += [(b, None) for b in range(nb * i, 0, -1)]
                    kl += [(0, None)]
                    tiles.append((NMETA + TB * i, TB, kl))
            else:
                kl = [(PAST // 128, 0)] + [(b, None) for b in range(PAST // 128 - 1, -1, -1)]
                tiles.append((0, DEC, kl))

            units = []
            for ti, (qn0, T, kl) in enumerate(tiles):
                for h in range(8):
                    for bi, (blk, mr) in enumerate(kl):
                        units.append((ti, qn0, T, h, bi, blk, mr, len(kl)))
            msb3 = v3(msbT.ap, 8)
            ubase = ucnt[0]
            ucnt[0] += len(units)

            def clo(mr, T):
                return 128 * mr if (mr is not None and T == TB) else 0

            def st0(ui):
                ti, qn0, T, h, bi, blk, mr, nblk = units[ui]
                u = ubase + ui
                qs = QTs[ti % 2]
                qs3 = v3(qs.ap, 4)
                if h == 0 and bi == 0:
                    goff = SEQ_OFF[sidx] + qn0
                    mr_ = mrTs2[ti % 2]
                    dma("sp", qs3[:, :, 0:T], qt_scr[:, :, goff:goff + T].rearrange("a p n -> p a n"), [], [qs.r], qs.r)
                    dma("sp", v3(mr_.ap, 4)[:, :, 0:T], mrt_scr[:, :, goff:goff + T].rearrange("a p n -> p a n"),
                        [], [mr_.r], mr_.r)
                p, jj = h // 2, h % 2
                hp0 = jj * 64
                n0, nk = blocks[blk]
                zb = ZB[u % 2]
                cl = clo(mr, T)
                mm(zb.ap[0:nk, cl:T], KT3[hp0:hp0 + 64, p, n0:n0 + nk], qs3[hp0:hp0 + 64, p, cl:T], True, True,
                   [KT.r, qs.r], [zb.r], grp=cur_grp[0])

            def st1(ui):
                ti, qn0, T, h, bi, blk, mr, nblk = units[ui]
                u = ubase + ui
                n0, nk = blocks[blk]
                zb = ZB[u % 2]
                es_ = e_sb[u % NEL]
                cl = clo(mr, T)
                act(es_.ap[0:nk, cl:T], zb.ap[0:nk, cl:T], AF.Exp, [zb.r], [es_.r], scale=0.125)
                if mr is not None:
                    c0 = 384 - 128 * mr
                    tt("dve", es_.ap[0:nk, cl:T], es_.ap[0:nk, cl:T], maskb.ap[0:nk, c0 + cl:c0 + T], ALU.mult,
                       [es_.r, maskb.r], [es_.r])

            def st1b(ui):
                ti, qn0, T, h, bi, blk, mr, nblk = units[ui]
                u = ubase + ui
                n0, nk = blocks[blk]
                es_ = e_sb[u % NEL]
                sps = sp_sb[u % NSL]
                cl = clo(mr, T)
                act(sps.ap[0:nk, cl:T], es_.ap[0:nk, cl:T], AF.Ln, [es_.r], [sps.r], bias=1.0)

            def st2(ui):
                ti, qn0, T, h, bi, blk, mr, nblk = units[ui]
                u = ubase + ui
                n0, nk = blocks[blk]
                cb = CB[u % 2]
                sps = sp_sb[u % NSL]
                cl = clo(mr, T)
                mm(cb.ap[0:nk, cl:T], tri.ap[0:nk, 0:nk], sps.ap[0:nk, cl:T], True, bi == 0, [tri.r, sps.r], [cb.r],
                   grp=cur_grp[0])
                if bi > 0:
                    sr = srun[(bi - 1) % 2]
                    mm(cb.ap[0:nk, cl:T], ones.ap[0:128, 0:nk], sr.ap[0:128, cl:T], False, True, [ones.r, sr.r], [cb.r],
                       grp=cur_grp[0])
                if bi + 1 < nblk:
                    if bi == 0:
                        mset("pool", srun[0].ap[:, 0:T], 0.0, [srun[0].r])
                        if cl > 0:
                            mset("pool", srun[1].ap[:, 0:T], 0.0, [srun[1].r])
                        cpy("pool", srun[0].ap[0:nk, cl:T], sps.ap[0:nk, cl:T], [sps.r], [srun[0].r])
                    else:
                        so, sn_ = srun[(bi - 1) % 2], srun[bi % 2]
                        assert nk == 128
                        tt("pool", sn_.ap[:, cl:T], so.ap[:, cl:T], sps.ap[0:nk, cl:T], ALU.add, [so.r, sps.r], [sn_.r])

            def st3(ui):
                ti, qn0, T, h, bi, blk, mr, nblk = units[ui]
                u = ubase + ui
                n0, nk = blocks[blk]
                cb = CB[u % 2]
                gs = g_sb[u % 3]
                cl = clo(mr, T)
                act(gs.ap[0:nk, cl:T], cb.ap[0:nk, cl:T], AF.Exp, [cb.r], [gs.r], scale=-1.0)

            def st3b(ui):
                ti, qn0, T, h, bi, blk, mr, nblk = units[ui]
                u = ubase + ui
                n0, nk = blocks[blk]
                es_ = e_sb[u % NEL]
                gs = g_sb[u % 3]
                as_ = a_sb[u % 3]
                cl = clo(mr, T)
                if cl > 0 and bi == 0:
                    mset("pool", as_.ap[0:nk, 0:cl], 0.0, [as_.r])
                tt("dve", as_.ap[0:nk, cl:T], es_.ap[0:nk, cl:T], gs.ap[0:nk, cl:T], ALU.mult, [es_.r, gs.r], [as_.r])

            def st4(ui):
                ti, qn0, T, h, bi, blk, mr, nblk = units[ui]
                u = ubase + ui
                n0, nk = blocks[blk]
                as_ = a_sb[u % 3]
                ob = OB[h % 2]
                cl = 0 if bi == 0 else clo(mr, T)
                mm(ob.ap[0:64, cl:T], Vt3[0:nk, blk, h * 64:(h + 1) * 64], as_.ap[0:nk, cl:T],
                   bi == 0, bi == nblk - 1, [Vt.r, as_.r], [ob.r], grp=cur_grp[0])
                if bi != nblk - 1:
                    return
                ot = OTs[h % 2]
                cpy("act", ot.ap[:, 0:T], ob.ap[0:64, 0:T], [ob.r], [ot.r])
                act(sq_sb.ap[:, 0:T], ob.ap[0:64, 0:T], AF.Square, [ob.r], [sq_sb.r])
                mm(MSB.ap[0:64, 0:T], od64.ap[0:64, 0:64], sq_sb.ap[:, 0:T], True, True, [od64.r, sq_sb.r], [MSB.r])
                act(rt_sb.ap[:, 0:T], MSB.ap[0:64, 0:T], AF.Sqrt, [MSB.r], [rt_sb.r], bias=EPS)
                recip(rt_sb.ap[:, 0:T], rt_sb.ap[:, 0:T], [rt_sb.r], [rt_sb.r])
                tt("dve", msb3[:, h, 0:T], ot.ap[:, 0:T], rt_sb.ap[:, 0:T], ALU.mult, [ot.r, rt_sb.r], [msbT.r])
                if h != 7:
                    return
                goff = SEQ_OFF[sidx] + qn0
                mr_ = mrTs2[ti % 2]
                nsub = (T + 127) // 128
                for sb_ in range(nsub):
                    t0 = sb_ * 128
                    ts_ = min(128, T - t0)
                    xx = xt2[sb_ % 2]
                    hh = h2s[sb_ % 2]
                    dma("sp", xx.ap[0:ts_, :], x_rows(kind, si, qn0 + t0, ts_), [], [xx.r], xx.r)
                    for cg in range(2):
                        b = MSB if cg == 0 else OB[1]
                        for hd in range(8):
                            mm(b.ap[0:ts_, :], msb3[:, hd, t0:t0 + ts_], wos3[:, hd, cg * 512:(cg + 1) * 512],
                               hd == 0, False, [msbT.r, wo_sb.r], [b.r])
                        for hd in range(4):
                            mm(b.ap[0:ts_, :], v3(mr_.ap, 4)[:, hd, t0:t0 + ts_], wor3[:, hd, cg * 512:(cg + 1) * 512],
                               False, hd == 3, [mr_.r, wo_ret.r], [b.r])
                        tt("dve", hh.ap[0:ts_, cg * 512:(cg + 1) * 512], xx.ap[0:ts_, cg * 512:(cg + 1) * 512],
                           b.ap[0:ts_, :], ALU.add, [xx.r, b.r], [hh.r])
                    dma("sp", h2_scr[goff + t0:goff + t0 + ts_, :], hh.ap[0:ts_, :], [hh.r], [], hh.r)

            stages = (st0, st1, st1b, st2, st3, st3b, st4)
            cur_grp = [None]
            for it in range(len(units) + len(stages) - 1):
                cur_grp[0] = ("a2", sidx, it)
                for sidx_ in range(len(stages) - 1, -1, -1):
                    ui = it - sidx_
                    if 0 <= ui < len(units):
                        stages[sidx_](ui)
                for _ in range(NFILL):
                    mm(TP.ap[:, 0:1024].bitcast(F32), tri.ap[:, :], maskb.ap[:, 0:512], True, True, [tri.r, maskb.r], [TP.r],
                       grp=cur_grp[0])

        for sidx, (kind, si) in enumerate(seqs if STAGE >= 4 else []):
            a2_seq(kind, si, sidx)

        S.barrier()
        AB.off = AB_BASE
        w_up_bf = AB.alloc("w_up_bf", 8 * 2 * DFF)
        w_dn_bf = AB.alloc("w_dn_bf", 22 * D)
        gcf = AFa.alloc("gcf", 8)
        cwt = AFa.alloc("cwt", NFC * 4)
        cw3 = v3(cwt.ap, NFC)
        gfin = AFa.alloc("gfin", D)
        carry = AFa.alloc("carry", NFC * 2)
        car3 = v3(carry.ap, NFC)
        xb = [AFa.alloc("xb0", D), AFa.alloc("xb1", D)]
        junkb = AFa.alloc("junkb", D)
        statb = AFa.alloc("statb", 16)
        hnb = AB.alloc("hnb", D)
        hT2 = AB.alloc("hT2", 8 * TB)
        gvT = AB.alloc("gvT", 22 * TB)
        cgate = [AFa.alloc("cgate0", TB), AFa.alloc("cgate1", TB)]
        B_MARK = AB.off
        wstage = [AFa.alloc("wst0c", 1408), AFa.alloc("wst1c", 1408)]
        dma("sp", gcf.ap, g_ffn.rearrange("(c p) -> p c", p=128), [], [gcf.r], gcf.r, slow=True)
        wu3 = v3(w_up_bf.ap, 8)
        wd3 = v3(w_dn_bf.ap, 22)
        for k in range(8):
            load_weight(Buf(wu3[:, k, :], w_up_bf.r), 128, w_up[k * 128:(k + 1) * 128, :], 2 * DFF,
                        gcf.ap[:, k:k + 1], gcf.r, piece=1408)
        for k in range(22):
            load_weight(Buf(wd3[:, k, :], w_dn_bf.r), 128, w_down[k * 128:(k + 1) * 128, :], D, None, None, piece=1024)
        for i in range(3):
            dma("sp", cw3[:, :, i], conv_w[i].rearrange("(c p) -> p c", p=128), [], [cwt.r], cwt.r, slow=True)
        dma("sp", cw3[:, :, 3], conv_b.rearrange("(c p) -> p c", p=128), [], [cwt.r], cwt.r, slow=True)
        dma("sp", gfin.ap, g_fin.partition_broadcast(128), [], [gfin.r], gfin.r)
        S.barrier()
        AB.off = B_MARK
        NUE = 4
        uext = [AFa.alloc(f"uext{i}", TB + 2) for i in range(NUE)]
        t1b = [AFa.alloc(f"t1b{i}", TB) for i in range(NUE)]
        UB = [pbank[0], pbank[1], pbank[2]]
        FB = [pbank[3], pbank[4]]
        hT23 = v3(hT2.ap, 8)
        gv3 = v3(gvT.ap, 22)
        ub_i = [0]
        deferred_b = []

        xpb = AFa.alloc("xpb", D)
        btiles = []
        for sidx, (kind, si) in enumerate(seqs if STAGE >= 6 else []):
            tl = ([(0, NMETA)] + [(NMETA + TB * i, TB) for i in range(SEQ // TB)]) if kind == "p" else [(0, DEC)]
            for ti, (n0, T) in enumerate(tl):
                btiles.append((kind, si, sidx, n0, T, ti == 0, ti == len(tl) - 1))

        def b_nsub(tile):
            return (tile[4] + 127) // 128

        def b_prep_rms(tile, sb_):
            kind, si, sidx, n0, T, fst, lst = tile
            goff = SEQ_OFF[sidx] + n0
            t0 = sb_ * 128
            ts_ = min(128, T - t0)
            dma("sp", xpb.ap[0:ts_, :], h2_scr[goff + t0:goff + t0 + ts_, :], [], [xpb.r], xpb.r)
            rms_rstd(xpb, ts_, junkb, statb)
            ts1("dve", hnb.ap[0:ts_, :], xpb.ap[0:ts_, :], statb.ap[0:ts_, 1:2], None, ALU.mult, None,
                [xpb.r, statb.r], [hnb.r])

        def b_prep_tr(tile, sb_):
            kind, si, sidx, n0, T, fst, lst = tile
            t0 = sb_ * 128
            ts_ = min(128, T - t0)
            tp3 = transposes_to(None, hnb, ts_, 8)
            cpy("act", hT23[:, :, t0:t0 + ts_], tp3[:, :, 0:ts_], [TP.r], [hT2.r])

        def b_up(tile):
            kind, si, sidx, n0, T, fst, lst = tile
            if fst:
                if kind == "p":
                    mset("pool", carry.ap, 0.0, [carry.r])
                else:
                    for r_ in range(2):
                        dma("sp", car3[:, :, r_], sconv[r_].rearrange("(c p) -> p c", p=128), [], [carry.r], carry.r,
                            slow=True)
            for fc in range(22):
                for which in range(2):
                    f = fc + 22 * which
                    ub = UB[ub_i[0] % 3]
                    ue = uext[ub_i[0] % NUE]
                    t1_ = t1b[ub_i[0] % NUE]
                    cgt = cgate[fc % 2]
                    ub_i[0] += 1
                    for k in range(8):
                        mm(ub.ap[:, 0:T], wu3[:, k, f * 128:(f + 1) * 128], hT23[:, k, 0:T], k == 0, k == 7,
                           [w_up_bf.r, hT2.r], [ub.r])
                    cpy("act", ue.ap[:, 0:2], car3[:, f, :], [carry.r], [ue.r])
                    cpy("act", ue.ap[:, 2:T + 2], ub.ap[:, 0:T], [ub.r], [ue.r])
                    cpy("act", car3[:, f, :], ue.ap[:, T:T + 2], [ue.r], [carry.r])
                    ts1("pool", t1_.ap[:, 0:T], ue.ap[:, 0:T], cw3[:, f, 0:1], cw3[:, f, 3:4], ALU.mult, ALU.add,
                        [ue.r, cwt.r], [t1_.r])
                    stt("dve", t1_.ap[:, 0:T], ue.ap[:, 1:T + 1], cw3[:, f, 1:2], t1_.ap[:, 0:T], ALU.mult, ALU.add,
                        [ue.r, cwt.r, t1_.r], [t1_.r])

                    def mk_c(ue=ue, t1_=t1_, f=f, T=T):
                        stt("dve", t1_.ap[:, 0:T], ue.ap[:, 2:T + 2], cw3[:, f, 2:3], t1_.ap[:, 0:T],
                            ALU.mult, ALU.add, [ue.r, cwt.r, t1_.r], [t1_.r])

                    def mk_silu(cgt=cgt, t1_=t1_, T=T):
                        act(cgt.ap[:, 0:T], t1_.ap[:, 0:T], AF.Silu, [t1_.r], [cgt.r])

                    def mk_mult(cgt=cgt, t1_=t1_, fc=fc, T=T):
                        tt("dve", gv3[:, fc, 0:T], t1_.ap[:, 0:T], cgt.ap[:, 0:T], ALU.mult, [t1_.r, cgt.r], [gvT.r])

                    deferred_b.append([1, mk_c])
                    deferred_b.append([2, mk_silu if which == 0 else mk_mult])
                    for ent in list(deferred_b):
                        if ent[0] == 0:
                            ent[1]()
                            deferred_b.remove(ent)
                        else:
                            ent[0] -= 1
            while deferred_b:
                for ent in list(deferred_b):
                    if ent[0] == 0:
                        ent[1]()
                        deferred_b.remove(ent)
                    else:
                        ent[0] -= 1

        def b_down(tile, sb_):
            kind, si, sidx, n0, T, fst, lst = tile
            goff = SEQ_OFF[sidx] + n0
            t0 = sb_ * 128
            ts_ = min(128, T - t0)
            xx = xb[sb_ % 2]
            dma("sp", xx.ap[0:ts_, :], h2_scr[goff + t0:goff + t0 + ts_, :], [], [xx.r], xx.r)
            for cg in range(2):
                b = FB[cg]
                for kc in range(22):
                    mm(b.ap[0:ts_, :], gv3[:, kc, t0:t0 + ts_], wd3[:, kc, cg * 512:(cg + 1) * 512],
                       kc == 0, kc == 21, [gvT.r, w_dn_bf.r], [b.r])
                tt("dve", xx.ap[0:ts_, cg * 512:(cg + 1) * 512], xx.ap[0:ts_, cg * 512:(cg + 1) * 512],
                   b.ap[0:ts_, :], ALU.add, [xx.r, b.r], [xx.r])
            rms_rstd(xx, ts_, junkb, statb)
            stt("dve", xx.ap[0:ts_, :], xx.ap[0:ts_, :], statb.ap[0:ts_, 1:2], gfin.ap[0:ts_, :],
                ALU.mult, ALU.mult, [xx.r, statb.r, gfin.r], [xx.r])
            if kind == "p":
                f0 = n0 - NMETA + t0
                dst = y_p[si, f0:f0 + ts_, :]
            else:
                dst = y_s[n0 + t0:n0 + t0 + ts_, :]
            dma("sp", dst, xx.ap[0:ts_, :], [xx.r], [], xx.r)

        def b_fin(tile):
            kind, si, sidx, n0, T, fst, lst = tile
            if not lst:
                return
            for r_ in range(2):
                if kind == "p":
                    dst = conv_p[si, r_].rearrange("(c p) -> p c", p=128)
                else:
                    dst = conv_s[r_].rearrange("(c p) -> p c", p=128)
                dma("sp", dst, car3[:, :, r_], [carry.r], [], carry.r, slow=True)

        if btiles:
            for k in range(b_nsub(btiles[0])):
                b_prep_rms(btiles[0], k)
                b_prep_tr(btiles[0], k)
        for i_, tile in enumerate(btiles):
            b_up(tile)
            nxt = btiles[i_ + 1] if i_ + 1 < len(btiles) else None
            is_meta = tile[0] == "p" and tile[3] == 0
            ns_cur = 0 if is_meta else b_nsub(tile)
            ns_nxt = b_nsub(nxt) if nxt is not None else 0
            for k in range(max(ns_cur, ns_nxt)):
                if k < ns_nxt:
                    b_prep_rms(nxt, k)
                if k < ns_cur:
                    b_down(tile, k)
                if k < ns_nxt:
                    b_prep_tr(nxt, k)
            b_fin(tile)
        out_res.extend([xb[0].r, xb[1].r, carry.r])
        S.emit(final_wait_res=out_res)
    return nc, consts


_CACHE = {}


def kernel(x_prompt, x_sample, cache_sb_k, cache_sb_v, state_ret, state_conv, meta_tokens, g_norm_mix, w_in,
           g_sb_out, g_ret_out, w_out, g_norm_ffn, w_up, conv_w, conv_b, w_down, g_norm_final):
    if "nc" not in _CACHE:
        _CACHE["nc"] = build_program()
    nc, consts = _CACHE["nc"]
    f = lambda a: np.ascontiguousarray(np.asarray(a, dtype=np.float32))
    x_prompt, x_sample = f(x_prompt), f(x_sample)
    cache_sb_k, cache_sb_v = f(cache_sb_k), f(cache_sb_v)
    state_ret, state_conv = f(state_ret), f(state_conv)
    shared = {
        "meta": f(meta_tokens), "g_mix": f(g_norm_mix), "w_in": f(w_in), "g_sbo": f(g_sb_out),
        "g_reto": f(g_ret_out), "w_out": f(w_out), "g_ffn": f(g_norm_ffn), "w_up": f(w_up),
        "conv_w": f(conv_w), "conv_b": f(conv_b), "w_down": f(w_down), "g_fin": f(g_norm_final),
    }
    shared.update(consts)
    in_maps = []
    for c in range(NCORES):
        m = dict(shared)
        m["xp"] = x_prompt[2 * c:2 * c + 2]
        m["xs"] = x_sample[c]
        m["ck"] = cache_sb_k[c]
        m["cv"] = cache_sb_v[c]
        m["sret"] = state_ret[c]
        m["sconv"] = state_conv[c]
        in_maps.append(m)
    res = run_bass_kernel_spmd(nc, in_maps, core_ids=list(range(NCORES)))
    rs = res.results
    cat = lambda k: np.concatenate([np.asarray(r[k], dtype=np.float32) for r in rs], axis=0)
    stk = lambda k: np.stack([np.asarray(r[k], dtype=np.float32) for r in rs], axis=0)
    return (cat("y_p"), stk("y_s"), cat("sbk_p"), cat("sbv_p"), cat("ret_p"), cat("conv_p"),
            stk("sbk_s"), stk("sbv_s"), stk("ret_s"), stk("conv_s"))
```

```python
import numpy as np
import concourse.bass as bass
import concourse.mybir as mybir
from concourse.bass_utils import run_bass_kernel_spmd

F32 = mybir.dt.float32
BF16 = mybir.dt.bfloat16
AF = mybir.ActivationFunctionType
ALU = mybir.AluOpType

D = 1024
NMETA = 16
SEQ = 4096
LP = NMETA + SEQ
DEC = 64
PAST = 4096
INW = 3584
DFF = 2816
NFC = 44
EPS = 1e-5
NCORES = 8
LTOT = 2 * LP + DEC
SEQ_OFF = (0, LP, 2 * LP)
TB = 512

COMPUTE = ("pe", "act", "dve", "pool")
SEM_LIMIT = 30000


class Res:
    __slots__ = ("name", "last_w", "readers", "sems", "dcount")

    def __init__(self, name):
        self.name = name
        self.last_w = None
        self.readers = []
        self.sems = None
        self.dcount = 0


class Op:
    __slots__ = ("idx", "eng", "fn", "deps", "dma_res", "needs_inc", "ticket", "grp")

    def __init__(self, idx, eng, fn, dma_res):
        self.idx = idx
        self.eng = eng
        self.fn = fn
        self.deps = {}
        self.dma_res = dma_res
        self.needs_inc = False
        self.ticket = None
        self.grp = None


class Sched:
    def __init__(self, nc):
        self.nc = nc
        self.ops = []
        self.n_res = 0
        self.last_of = {}
        self.dma_since_bar = []

    def res(self, name=None):
        self.n_res += 1
        return Res(name or f"r{self.n_res}")

    def op(self, eng, fn, reads=(), writes=(), dma_res=None, grp=None):
        idx = len(self.ops)
        o = Op(idx, eng, fn, dma_res)
        o.grp = grp
        for r in reads:
            if r.last_w is not None:
                o.deps[r.last_w] = "RAW"
            r.readers.append(idx)
        for w in writes:
            if w.last_w is not None:
                o.deps.setdefault(w.last_w, "WAW")
            for rd in w.readers:
                if rd != idx:
                    o.deps.setdefault(rd, "WAR")
            w.last_w = idx
            w.readers = []
        keep = {}
        for d, kind in o.deps.items():
            dop = self.ops[d]
            if dop.dma_res is None and dma_res is None and dop.eng == eng:
                if eng == "pe":
                    continue
            keep[d] = kind
        o.deps = keep
        for d in keep:
            self.ops[d].needs_inc = True
        self.ops.append(o)
        if dma_res is None:
            self.last_of[eng] = idx
        else:
            self.dma_since_bar.append(idx)
        return o

    def barrier(self):
        deps = set(self.last_of.values()) | set(self.dma_since_bar)
        self.dma_since_bar = []
        for eng in ("sp",) + COMPUTE:
            idx = len(self.ops)
            o = Op(idx, eng, None, None)
            for d in deps:
                dop = self.ops[d]
                if dop.dma_res is None and dop.eng == eng:
                    continue
                o.deps[d] = "BAR"
                dop.needs_inc = True
            self.ops.append(o)

    def emit(self, final_wait_res=()):
        nc = self.nc
        ops = self.ops
        sems = []

        def new_sem(name):
            h = nc.alloc_semaphore(name=name)
            sems.append(h)
            return h

        eng_sem = {e: [new_sem(f"s_{e}0")] for e in COMPUTE}
        eng_cnt = {e: 0 for e in COMPUTE}
        for o in ops:
            if o.dma_res is not None:
                r = o.dma_res
                if r.sems is None:
                    r.sems = [new_sem(f"d_{r.name}")]
                    r.dcount = 0
                if r.dcount + 16 > SEM_LIMIT:
                    r.sems.append(new_sem(f"d_{r.name}_{len(r.sems)}"))
                    r.dcount = 0
                r.dcount += 16
                o.ticket = (r.sems[-1], r.dcount)
            elif o.needs_inc and o.fn is not None:
                e = o.eng
                if eng_cnt[e] + 1 > SEM_LIMIT:
                    eng_sem[e].append(new_sem(f"s_{e}{len(eng_sem[e])}"))
                    eng_cnt[e] = 0
                eng_cnt[e] += 1
                o.ticket = (eng_sem[e][-1], eng_cnt[e])
        self.n_sems = len(sems)
        by_eng = {}
        for o in ops:
            by_eng.setdefault(o.eng, []).append(o)
        final = []
        for r in final_wait_res:
            if r.sems is not None:
                final.append((r.sems[-1], r.dcount))

        def run(eh, ename, is_last=False):
            known = {}
            q = by_eng.get(ename, [])
            for qi, o in enumerate(q):
                need = {}
                srcs = [o]
                if o.grp is not None and (qi == 0 or q[qi - 1].grp != o.grp):
                    qj = qi + 1
                    while qj < len(q) and q[qj].grp == o.grp:
                        srcs.append(q[qj])
                        qj += 1
                for so_ in srcs:
                    for d in so_.deps:
                        t = ops[d].ticket
                        if t is None:
                            continue
                        sem, val = t
                        k = id(sem)
                        if k not in need or need[k][1] < val:
                            need[k] = (sem, val)
                for k, (sem, val) in need.items():
                    if known.get(k, 0) >= val:
                        continue
                    known[k] = val
                    eh.wait_ge(sem, val)
                if o.fn is None:
                    continue
                ins = o.fn(eh)
                if o.ticket is not None:
                    sem, val = o.ticket
                    ins.then_inc(sem, 16 if o.dma_res is not None else 1)
            if is_last:
                for sem, val in final:
                    eh.wait_ge(sem, val)

        with nc.Block() as block:
            @block.sync
            def _(e):
                run(e, "sp", is_last=True)

            @block.tensor
            def _(e):
                run(e, "pe")

            @block.scalar
            def _(e):
                run(e, "act")

            @block.vector
            def _(e):
                run(e, "dve")

            @block.gpsimd
            def _(e):
                run(e, "pool")


class Buf:
    __slots__ = ("ap", "r")

    def __init__(self, ap, r):
        self.ap = ap
        self.r = r


class Arena:
    def __init__(self, S, tensor, ncols):
        self.S = S
        self.t = tensor
        self.ncols = ncols
        self.off = 0
        self.hi = 0

    def _take(self, name, cols_bf):
        cols_al = (cols_bf + 31) // 32 * 32
        assert self.off + cols_al <= self.ncols, (name, self.off, cols_al, self.ncols)
        o = self.off
        self.off += cols_al
        self.hi = max(self.hi, self.off)
        return o


class ArenaView:
    def __init__(self, arena, f32):
        self.a = arena
        self.f32 = f32

    @property
    def off(self):
        return self.a.off

    @off.setter
    def off(self, v):
        self.a.off = v

    def alloc(self, name, cols, parts=128):
        a = self.a
        if self.f32:
            o = a._take(name, 2 * cols)
            ap = a.t[0:parts, o:o + 2 * cols].bitcast(F32)
        else:
            o = a._take(name, cols)
            ap = a.t[0:parts, o:o + cols]
        return Buf(ap, a.S.res(name))


def _consts():
    c = {}
    c["c_ident"] = np.eye(128, dtype=np.float32)
    j = np.arange(128)
    c["c_tri"] = (j[:, None] >= j[None, :]).astype(np.float32)
    c["c_ones"] = np.ones((128, 128), np.float32)
    tp = np.arange(896)
    c["c_mask"] = (j[:, None] < (tp[None, :] - 384)).astype(np.float32)
    hh = np.arange(4, dtype=np.float64)
    gam = 1.0 - 2.0 ** (-5.0 - hh)
    logg = np.log(gam)
    diff = j[None, :] - j[:, None]
    dec = np.where(diff[None] >= 0, np.exp(np.maximum(diff[None], 0) * logg[:, None, None]), 0.0)
    c["c_dec"] = np.ascontiguousarray(dec.transpose(1, 0, 2)).astype(np.float32)
    cg = np.exp((j[None, :] + 1.0) * logg[:, None])
    c["c_crossg"] = np.ascontiguousarray(np.broadcast_to(cg[None], (128, 4, 128))).astype(np.float32)
    kd = np.zeros((128, 3, 4), np.float64)
    for li, l in enumerate((16, 64, 128)):
        for h in range(4):
            kd[:l, li, h] = np.exp((l - 1.0 - j[:l]) * logg[h])
    c["c_kd"] = np.repeat(kd.reshape(128, 12), 128, axis=1).astype(np.float32)
    pos = np.concatenate([np.arange(LP) - NMETA, PAST + np.arange(DEC)]).astype(np.float32)
    half = 64
    inv = (10000.0 ** (-np.arange(half, dtype=np.float32) / half)).astype(np.float32)
    ang = (pos[:, None] * inv[None, :]).astype(np.float32)
    cs, sn = np.cos(ang.astype(np.float64)), np.sin(ang.astype(np.float64))
    s = 128.0 ** -0.5
    t4 = lambda a: np.tile(a, (1, 4))
    c["c_rope"] = np.concatenate([t4(cs), t4(sn), t4(cs * s), t4(sn * s)], axis=1).astype(np.float32)
    gl = {l: [float(np.exp(l * logg[h])) for h in range(4)] for l in (16, 64, 128)}
    return c, gl


def build_program():
    consts, GL = _consts()
    import os
    STAGE = int(os.environ.get('KSTAGE', '9'))
    KSUB = float(os.environ.get('KSUB', '99'))
    KRET = int(os.environ.get('KRET', '7'))
    nc = bass.Bass("TRN2", target_bir_lowering=False)

    def din(name, shape):
        return nc.dram_tensor(name, list(shape), F32, kind="ExternalInput").ap()

    def dout(name, shape):
        return nc.dram_tensor(name, list(shape), F32, kind="ExternalOutput").ap()

    xp = din("xp", (2, SEQ, D))
    xs = din("xs", (DEC, D))
    ck = din("ck", (8, PAST, 64))
    cv = din("cv", (8, PAST, 64))
    sret = din("sret", (4, 128, 128))
    sconv = din("sconv", (2, 5632))
    meta = din("meta", (NMETA, D))
    g_mix = din("g_mix", (D,))
    w_in = din("w_in", (D, INW))
    g_sbo = din("g_sbo", (512,))
    g_reto = din("g_reto", (512,))
    w_out = din("w_out", (D, D))
    g_ffn = din("g_ffn", (D,))
    w_up = din("w_up", (D, 2 * DFF))
    conv_w = din("conv_w", (3, 2 * DFF))
    conv_b = din("conv_b", (2 * DFF,))
    w_down = din("w_down", (DFF, D))
    g_fin = din("g_fin", (D,))
    cdr = {k: din(k, v.shape) for k, v in consts.items()}

    y_p = dout("y_p", (2, SEQ, D))
    y_s = dout("y_s", (DEC, D))
    sbk_p = dout("sbk_p", (2, 8, LP, 64))
    sbv_p = dout("sbv_p", (2, 8, LP, 64))
    ret_p = dout("ret_p", (2, 4, 128, 128))
    conv_p = dout("conv_p", (2, 2, 5632))
    sbk_s = dout("sbk_s", (8, DEC, 64))
    sbv_s = dout("sbv_s", (8, DEC, 64))
    ret_s = dout("ret_s", (4, 128, 128))
    conv_s = dout("conv_s", (2, 5632))

    qt_scr = nc.dram_tensor("qt_scr", [4, 128, LTOT], BF16).ap()
    mrt_scr = nc.dram_tensor("mrt_scr", [4, 128, LTOT], BF16).ap()
    h2_scr = nc.dram_tensor("h2_scr", [LTOT, D], F32).ap()

    S = Sched(nc)
    out_res = []

    seqs = [("p", 0), ("p", 1), ("s", 0)]

    def seq_chunks(kind):
        if kind == "p":
            return [(0, NMETA)] + [(NMETA + 128 * c, 128) for c in range(SEQ // 128)]
        return [(0, DEC)]

    def x_rows(kind, si, n0, l):
        if kind == "p":
            if n0 == 0:
                return meta[0:l, :]
            return xp[si, n0 - NMETA:n0 - NMETA + l, :]
        return xs[n0:n0 + l, :]

    def rope_row0(kind, n0):
        return n0 if kind == "p" else LP + n0

    NARENA = 106368
    from contextlib import ExitStack
    with ExitStack() as es:
        arena_t = es.enter_context(nc.sbuf_tensor("arena", [128, NARENA], BF16))
        _arena = Arena(S, arena_t, NARENA)
        AB = ArenaView(_arena, False)
        AFa = ArenaView(_arena, True)
        pbank = []
        for i in range(7):
            t = es.enter_context(nc.psum_tensor(f"pb{i}", [128, 512], F32))
            pbank.append(Buf(t[:, :], S.res(f"pb{i}")))
        tpt = es.enter_context(nc.psum_tensor("tp", [128, 1024], BF16))
        TP = Buf(tpt[:, :], S.res("tp"))

        def dma(q, out_ap, in_ap, reads, writes, res, slow=False):
            if slow:
                fn = lambda e: e.dma_start(out=out_ap, in_=in_ap, allow_slow_non_contiguous=True)
            else:
                fn = lambda e: e.dma_start(out=out_ap, in_=in_ap)
            S.op(q, fn, reads=reads, writes=writes, dma_res=res)

        def mm(out_ap, lhsT, rhs, start, stop, reads, writes, grp=None):
            S.op("pe", lambda e: e.matmul(out_ap, lhsT=lhsT, rhs=rhs, start=start, stop=stop),
                 reads=reads, writes=writes, grp=grp)

        def tr(out_ap, in_ap, ident_ap, reads, writes):
            S.op("pe", lambda e: e.transpose(out_ap, in_ap, ident_ap), reads=reads, writes=writes)

        def act(out_ap, in_ap, func, reads, writes, bias=None, scale=None, accum=None):
            kw = {}
            if bias is not None:
                kw["bias"] = bias
            if scale is not None:
                kw["scale"] = scale
            if accum is not None:
                kw["accum_out"] = accum
            S.op("act", lambda e: e.activation(out=out_ap, in_=in_ap, func=func, **kw), reads=reads, writes=writes)

        def tt(eng, out_ap, in0, in1, op, reads, writes):
            S.op(eng, lambda e: e.tensor_tensor(out=out_ap, in0=in0, in1=in1, op=op), reads=reads, writes=writes)

        def ts1(eng, out_ap, in0, s1, s2, op0, op1, reads, writes):
            if s2 is None:
                fn = lambda e: e.tensor_scalar(out=out_ap, in0=in0, scalar1=s1, scalar2=None, op0=op0)
            else:
                fn = lambda e: e.tensor_scalar(out=out_ap, in0=in0, scalar1=s1, scalar2=s2, op0=op0, op1=op1)
            S.op(eng, fn, reads=reads, writes=writes)

        def stt(eng, out_ap, in0, scalar, in1, op0, op1, reads, writes):
            S.op(eng, lambda e: e.scalar_tensor_tensor(out=out_ap, in0=in0, scalar=scalar, in1=in1, op0=op0, op1=op1),
                 reads=reads, writes=writes)

        def cpy(eng, out_ap, in_ap, reads, writes):
            if eng == "act":
                S.op("act", lambda e: e.activation(out=out_ap, in_=in_ap, func=AF.Copy), reads=reads, writes=writes)
            else:
                S.op(eng, lambda e: e.tensor_copy(out=out_ap, in_=in_ap), reads=reads, writes=writes)

        def recip(out_ap, in_ap, reads, writes):
            S.op("dve", lambda e: e.reciprocal(out=out_ap, in_=in_ap), reads=reads, writes=writes)

        def mset(eng, ap, val, writes):
            S.op(eng, lambda e: e.memset(ap, val), writes=writes)

        def v3(ap, a):
            return ap.rearrange("p (a b) -> p a b", a=a)

        ident = AB.alloc("ident", 128)
        tri = AB.alloc("tri", 128)
        ones = AB.alloc("ones", 128)
        od128 = AB.alloc("od128", 128)
        od64 = AB.alloc("od64", 128)
        dma("pool", ident.ap, cdr["c_ident"], [], [ident.r], ident.r)
        dma("pool", tri.ap, cdr["c_tri"], [], [tri.r], tri.r)
        dma("pool", ones.ap, cdr["c_ones"], [], [ones.r], ones.r)
        ts1("pool", od128.ap, ones.ap, 1.0 / 128.0, None, ALU.mult, None, [ones.r], [od128.r])
        ts1("pool", od64.ap, ones.ap, 1.0 / 64.0, None, ALU.mult, None, [ones.r], [od64.r])
        AB_BASE = AB.off

        wstage = None

        def load_weight(dst, k_rows, src_rows_ap, ncols, gcol_ap, gres, piece=2048):
            c0 = 0
            i = 0
            while c0 < ncols:
                cw = min(piece, ncols - c0)
                st = wstage[load_weight.cnt % 2]
                load_weight.cnt += 1
                dma("sp", st.ap[0:k_rows, 0:cw], src_rows_ap[:, c0:c0 + cw], [], [st.r], st.r)
                on_act = (load_weight.cnt % 2 == 0)
                if gcol_ap is None:
                    cpy("act" if on_act else "dve", dst.ap[0:k_rows, c0:c0 + cw], st.ap[0:k_rows, 0:cw], [st.r], [dst.r])
                elif on_act:
                    act(dst.ap[0:k_rows, c0:c0 + cw], st.ap[0:k_rows, 0:cw], AF.Copy, [st.r, gres], [dst.r], scale=gcol_ap)
                else:
                    ts1("dve", dst.ap[0:k_rows, c0:c0 + cw], st.ap[0:k_rows, 0:cw],
                        gcol_ap, None, ALU.mult, None, [st.r, gres], [dst.r])
                c0 += cw
                i += 1
        load_weight.cnt = 0

        def rms_rstd(xt, l, junk, stat):
            mset("pool", stat.ap[0:l, 0:1], 0.0, [stat.r])
            act(junk.ap[0:l, 0:D], xt.ap[0:l, 0:D], AF.Square, [xt.r, stat.r], [junk.r, stat.r],
                scale=1.0 / 32.0, accum=stat.ap[0:l, 0:1])
            act(stat.ap[0:l, 1:2], stat.ap[0:l, 0:1], AF.Sqrt, [stat.r], [stat.r], bias=EPS)
            recip(stat.ap[0:l, 1:2], stat.ap[0:l, 1:2], [stat.r], [stat.r])

        def transposes_to(dst3, src, l, nblk, dst_reads_extra=()):
            tp3 = v3(TP.ap[:, 0:nblk * 128], nblk)
            for k in range(nblk):
                tr(tp3[:, k, 0:l], src.ap[0:l, k * 128:(k + 1) * 128], ident.ap[0:l, 0:l],
                   [src.r, ident.r], [TP.r])
            return tp3

        w_in_bf = AB.alloc("w_in_bf", 8 * INW)
        gcol = AFa.alloc("gcol", 8)
        wstage = [AFa.alloc("wst0", 2048), AFa.alloc("wst1", 2048)]
        dma("sp", gcol.ap, g_mix.rearrange("(c p) -> p c", p=128), [], [gcol.r], gcol.r, slow=True)
        w_in3 = v3(w_in_bf.ap, 8)
        for k in range(8 if STAGE >= 1 else 0):
            load_weight(Buf(w_in3[:, k, :], w_in_bf.r), 128, w_in[k * 128:(k + 1) * 128, :], INW,
                        gcol.ap[:, k:k + 1], gcol.r, piece=1792)

        xt = [AFa.alloc("xt0", D), AFa.alloc("xt1", D)]
        junk = AFa.alloc("junk", D)
        stat = AFa.alloc("stat", 16)
        hn = AB.alloc("hn", D)
        hT = AB.alloc("hT", 8 * 128)
        q_bf = AB.alloc("q_bf", 512)
        qT_sb = [AB.alloc("qT_sb0", 512), AB.alloc("qT_sb1", 512)]
        kst = [AFa.alloc("kst0", 512), AFa.alloc("kst1", 512)]
        vst = [AFa.alloc("vst0", 512), AFa.alloc("vst1", 512)]
        rqk32 = AFa.alloc("rqk32", 1024)
        ropet = AFa.alloc("ropet", 1024)
        rtmp = [AFa.alloc("rtmpa", 256), AFa.alloc("rtmpb", 256)]
        rqk_bf2 = [AB.alloc("rqk_bf0", 1024), AB.alloc("rqk_bf1", 1024)]
        rv_bf2 = [AB.alloc("rv_bf0", 512), AB.alloc("rv_bf1", 512)]
        kdec_bf = AB.alloc("kdec_bf", 512)
        sg322 = [AFa.alloc("sg32_0", 512), AFa.alloc("sg32_1", 512)]
        rqT = AB.alloc("rqT", 512)
        rkT = AB.alloc("rkT", 512)
        rqgT = AB.alloc("rqgT", 512)
        PT = AB.alloc("PT", 512)
        S32 = AFa.alloc("S32", 512)
        S_bf = AB.alloc("S_bf", 512)
        decT = AFa.alloc("decT", 512)
        crossg = AFa.alloc("crossg", 512)
        kdtab = AFa.alloc("kdtab", 1536)
        ostat = AFa.alloc("ostat", 16)
        mret_bf = AB.alloc("mret_bf", 512)
        mrT_sb = [AB.alloc("mrT_sb0", 512), AB.alloc("mrT_sb1", 512)]
        dma("sp", decT.ap, cdr["c_dec"].rearrange("j h i -> j (h i)"), [], [decT.r], decT.r)
        dma("sp", crossg.ap, cdr["c_crossg"].rearrange("j h i -> j (h i)"), [], [crossg.r], crossg.r)
        dma("sp", kdtab.ap, cdr["c_kd"], [], [kdtab.r], kdtab.r)

        R1, R2, R3 = pbank[4], pbank[5], pbank[6]
        mmb = [pbank[0], pbank[1], pbank[2], pbank[3]]
        mmi = [0]

        def next_bank():
            b = mmb[mmi[0] % len(mmb)]
            mmi[0] += 1
            return b

        chunk_id = [0]
        deferred = []

        def a1_load(kind, si, n0, l, slot):
            dma("sp", xt[slot].ap[0:l, :], x_rows(kind, si, n0, l), [], [xt[slot].r], xt[slot].r)

        def a1_front(kind, si, sidx, n0, l, slot, x, par, goff, rqk_bf, rv_bf, sg32):
            rms_rstd(x, l, junk, stat)
            ts1("dve", hn.ap[0:l, :], x.ap[0:l, :], stat.ap[0:l, 1:2], None, ALU.mult, None, [x.r, stat.r], [hn.r])
            yield
            tp3 = transposes_to(None, hn, l, 8)
            hT3 = v3(hT.ap, 8)
            cpy("act", hT3[:, :, 0:l], tp3[:, :, 0:l], [TP.r], [hT.r])
            r0 = rope_row0(kind, n0)
            dma("sp", ropet.ap[0:l, :], cdr["c_rope"][r0:r0 + l, :], [], [ropet.r], ropet.r)
            yield

            if KSUB < 2:
                return

            def proj(cg):
                b = next_bank()
                for k in range(8):
                    mm(b.ap[0:l, :], hT3[:, k, 0:l], w_in3[:, k, cg * 512:(cg + 1) * 512], k == 0, k == 7,
                       [hT.r, w_in_bf.r], [b.r])
                return b

            b = proj(0)
            cpy("act", q_bf.ap[0:l, :], b.ap[0:l, :], [b.r], [q_bf.r])
            yield
            tp3q = transposes_to(None, q_bf, l, 4)
            qs = qT_sb[par]
            cpy("dve", v3(qs.ap, 4)[:, :, 0:l], tp3q[:, :, 0:l], [TP.r], [qs.r])
            dma("sp", qt_scr[:, :, goff:goff + l].rearrange("a p n -> p a n"), v3(qs.ap, 4)[:, :, 0:l],
                [qs.r], [], qs.r)
            yield
            if KSUB < 3:
                return
            for cg, stg, outp, outs in ((1, kst[par], sbk_p, sbk_s), (2, vst[par], sbv_p, sbv_s)):
                b = proj(cg)
                cpy("act" if cg == 1 else "dve", stg.ap[0:l, :], b.ap[0:l, :], [b.r], [stg.r])
                if kind == "p":
                    dst = outp[si, :, n0:n0 + l, :].rearrange("h n d -> n h d")
                else:
                    dst = outs[:, n0:n0 + l, :].rearrange("h n d -> n h d")
                dma("sp", dst, stg.ap[0:l, :].rearrange("p (h d) -> p h d", h=8), [stg.r], [], stg.r)
                yield
            if KSUB < 4:
                return
            for cg in (3, 4):
                b = proj(cg)
                cpy("act", rqk32.ap[0:l, (cg - 3) * 512:(cg - 2) * 512], b.ap[0:l, :], [b.r], [rqk32.r])
                yield
            for qk in range(2):
                src = rqk32.ap[0:l, qk * 512:(qk + 1) * 512].rearrange("p (h t d) -> p h t d", h=4, t=2)
                dst = rqk_bf.ap[0:l, qk * 512:(qk + 1) * 512].rearrange("p (h t d) -> p h t d", h=4, t=2)
                cosb = v3(ropet.ap[0:l, qk * 512:qk * 512 + 256], 4)
                sinb = v3(ropet.ap[0:l, qk * 512 + 256:qk * 512 + 512], 4)
                ta = v3(rtmp[0].ap[0:l, :], 4)
                tb = v3(rtmp[1].ap[0:l, :], 4)
                t1, t2 = src[:, :, 0, :], src[:, :, 1, :]
                tt("dve", ta, t1, cosb, ALU.mult, [rqk32.r, ropet.r], [rtmp[0].r])
                tt("dve", tb, t2, sinb, ALU.mult, [rqk32.r, ropet.r], [rtmp[1].r])
                tt("dve", dst[:, :, 0, :], ta, tb, ALU.subtract, [rtmp[0].r, rtmp[1].r], [rqk_bf.r])
                yield
                tt("dve", ta, t1, sinb, ALU.mult, [rqk32.r, ropet.r], [rtmp[0].r])
                tt("dve", tb, t2, cosb, ALU.mult, [rqk32.r, ropet.r], [rtmp[1].r])
                tt("dve", dst[:, :, 1, :], ta, tb, ALU.add, [rtmp[0].r, rtmp[1].r], [rqk_bf.r])
                yield
            if KSUB < 5:
                return
            b = proj(5)
            cpy("act", rv_bf.ap[0:l, :], b.ap[0:l, :], [b.r], [rv_bf.r])
            yield
            b = proj(6)
            act(sg32.ap[0:l, :], b.ap[0:l, :], AF.Silu, [b.r], [sg32.r])

        def a1_chunk(kind, si, sidx, n0, l, slot, first, last, phase):
            li = {16: 0, 64: 1, 128: 2}[l]
            par = slot
            goff = SEQ_OFF[sidx] + n0
            x = xt[slot]
            rqk_bf = rqk_bf2[par]
            rv_bf = rv_bf2[par]
            sg32 = sg322[par]
            if phase == 0:
                yield from a1_front(kind, si, sidx, n0, l, slot, x, par, goff, rqk_bf, rv_bf, sg32)
                return
            hT3 = v3(hT.ap, 8)
            if KSUB < 6:
                return
            kd = kdtab.ap[0:l, li * 512:(li + 1) * 512]
            if KSUB >= 6.02:
                tt("dve", kdec_bf.ap[0:l, :], rqk_bf.ap[0:l, 512:1024], kd, ALU.mult,
                   [rqk_bf.r, kdtab.r], [kdec_bf.r])
            yield
            tp3r = transposes_to(None, rqk_bf, l, 8)
            cpy("act", v3(rqT.ap, 4)[:, :, 0:l], tp3r[:, 0:4, 0:l], [TP.r], [rqT.r])
            cpy("act", v3(rkT.ap, 4)[:, :, 0:l], tp3r[:, 4:8, 0:l], [TP.r], [rkT.r])
            if KSUB >= 6.05:
                tt("dve", v3(rqgT.ap, 4)[:, :, 0:l], v3(rqT.ap, 4)[:, :, 0:l], v3(crossg.ap, 4)[:, :, 0:l], ALU.mult,
                   [rqT.r, crossg.r], [rqgT.r])
            yield
            if KSUB < 6.2:
                return
            if first:
                if kind == "p":
                    mset("pool", S32.ap, 0.0, [S32.r])
                else:
                    dma("sp", v3(S32.ap, 4), sret.rearrange("h k v -> k h v"), [], [S32.r], S32.r)
                cpy("act", S_bf.ap, S32.ap, [S32.r], [S_bf.r])
                yield
            if KSUB < 6.4:
                return
            r1 = v3(R1.ap, 4)
            for h in range(4):
                mm(r1[0:l, h, 0:l], v3(rkT.ap, 4)[:, h, 0:l], v3(rqT.ap, 4)[:, h, 0:l], True, True,
                   [rkT.r, rqT.r], [R1.r])
            tt("dve", v3(PT.ap, 4)[0:l, :, 0:l], r1[0:l, :, 0:l], v3(decT.ap, 4)[0:l, :, 0:l], ALU.mult,
               [R1.r, decT.r], [PT.r])
            yield
            r2 = v3(R2.ap, 4)
            r3 = v3(R3.ap, 4)
            if KSUB < 6.6:
                return
            for h in range(4):
                mm(r2[0:l, h, :], v3(PT.ap, 4)[0:l, h, 0:l], rv_bf.ap[0:l, h * 128:(h + 1) * 128], True, False,
                   [PT.r, rv_bf.r], [R2.r])
                mm(r2[0:l, h, :], v3(rqgT.ap, 4)[:, h, 0:l], v3(S_bf.ap, 4)[:, h, :], False, True,
                   [rqgT.r, S_bf.r], [R2.r])
            yield
            if KSUB < 6.8:
                return
            for h in range(4):
                mm(r3[:, h, :], v3(kdec_bf.ap, 4)[0:l, h, :], rv_bf.ap[0:l, h * 128:(h + 1) * 128], True, True,
                   [kdec_bf.r, rv_bf.r], [R3.r])
            yield
            for h in range(4):
                ts1("dve", v3(S32.ap, 4)[:, h, :], v3(S32.ap, 4)[:, h, :], GL[l][h], None, ALU.mult, None,
                    [S32.r], [S32.r])
                tt("dve", v3(S32.ap, 4)[:, h, :], v3(S32.ap, 4)[:, h, :], r3[:, h, :], ALU.add,
                   [S32.r, R3.r], [S32.r])
                if h % 2 == 1:
                    yield
            cpy("act", S_bf.ap, S32.ap, [S32.r], [S_bf.r])
            if last and KSUB >= 6.95 and (KRET >> sidx) & 1:
                if os.environ.get("KDUMMY"):
                    stg = kst[par]
                    dstk = (sbk_p[si, :, n0:n0 + l, :] if kind == "p" else sbk_s[:, n0:n0 + l, :]).rearrange("h n d -> n h d")
                    dma("sp", dstk, stg.ap[0:l, :].rearrange("p (h d) -> p h d", h=8), [stg.r], [], stg.r)
                else:
                    for h in range(4):
                        dsth = ret_p[si, h] if kind == "p" else ret_s[h]
                        dma("sp", dsth, v3(S32.ap, 4)[:, h, :], [S32.r], [], S32.r)
            yield
            if KSUB < 7:
                return
            mset("pool", ostat.ap[0:l, 0:4], 0.0, [ostat.r])
            for h in range(4):
                act(junk.ap[0:l, h * 128:(h + 1) * 128], r2[0:l, h, :], AF.Square, [R2.r, ostat.r], [junk.r, ostat.r],
                    scale=float(128.0 ** -0.5), accum=ostat.ap[0:l, h:h + 1])
            act(ostat.ap[0:l, 4:8], ostat.ap[0:l, 0:4], AF.Sqrt, [ostat.r], [ostat.r], bias=EPS)
            recip(ostat.ap[0:l, 4:8], ostat.ap[0:l, 4:8], [ostat.r], [ostat.r])
            yield
            for h in range(4):
                stt("dve", mret_bf.ap[0:l, h * 128:(h + 1) * 128], r2[0:l, h, :], ostat.ap[0:l, 4 + h:5 + h],
                    sg32.ap[0:l, h * 128:(h + 1) * 128], ALU.mult, ALU.mult, [R2.r, ostat.r, sg32.r], [mret_bf.r])
            yield
            tp3m = transposes_to(None, mret_bf, l, 4)
            ms = mrT_sb[par]
            cpy("act", v3(ms.ap, 4)[:, :, 0:l], tp3m[:, :, 0:l], [TP.r], [ms.r])
            dma("sp", mrt_scr[:, :, goff:goff + l].rearrange("a p n -> p a n"), v3(ms.ap, 4)[:, :, 0:l],
                [ms.r], [], ms.r)

        work = []
        for sidx, (kind, si) in enumerate(seqs):
            ch = seq_chunks(kind)
            for ci, (n0, l) in enumerate(ch):
                work.append((kind, si, sidx, n0, l, ci == 0, ci == len(ch) - 1))
        if STAGE < 2:
            work = []
        else:
            a1_load(work[0][0], work[0][1], work[0][3], work[0][4], 0)
        def run_rr(gens):
            gens = list(gens)
            while gens:
                for g in list(gens):
                    try:
                        next(g)
                    except StopIteration:
                        gens.remove(g)

        for wi in range(len(work) + 1):
            gens = []
            if wi < len(work):
                kind, si, sidx, n0, l, first, last = work[wi]
                if wi + 1 < len(work):
                    nk_ = work[wi + 1]
                    a1_load(nk_[0], nk_[1], nk_[3], nk_[4], (wi + 1) % 2)
                gens.append(a1_chunk(kind, si, sidx, n0, l, wi % 2, first, last, 0))
            if wi >= 1:
                kind, si, sidx, n0, l, first, last = work[wi - 1]
                gens.append(a1_chunk(kind, si, sidx, n0, l, (wi - 1) % 2, first, last, 1))
            run_rr(gens)
        out_res.extend([kst[0].r, kst[1].r, vst[0].r, vst[1].r, S32.r])

        S.barrier()
        for dsth, srch in deferred:
            dma("sp", dsth, srch, [S32.r], [], S32.r)
        if deferred:
            S.barrier()
        AB.off = AB_BASE
        wo_sb = AB.alloc("wo_sb", 8 * D, parts=64)
        wo_ret = AB.alloc("wo_ret", 4 * D)
        KT = AB.alloc("KT", 4 * 4224)
        Vt = AB.alloc("Vt", 34 * 512)
        gcs = AFa.alloc("gcs", 8, parts=64)
        gcr = AFa.alloc("gcr", 8)
        wstage = [AFa.alloc("wst0b", 1024), AFa.alloc("wst1b", 1024)]
        dma("sp", gcs.ap, g_sbo.rearrange("(h p) -> p h", p=64), [], [gcs.r], gcs.r, slow=True)
        dma("sp", gcr.ap[:, 0:4], g_reto.rearrange("(h p) -> p h", p=128), [], [gcr.r], gcr.r, slow=True)
        wos3 = v3(wo_sb.ap, 8)
        wor3 = v3(wo_ret.ap, 4)
        for h in range(8):
            load_weight(Buf(wos3[:, h, :], wo_sb.r), 64, w_out[h * 64:(h + 1) * 64, :], D, gcs.ap[:, h:h + 1], gcs.r,
                        piece=1024)
        for h in range(4):
            load_weight(Buf(wor3[:, h, :], wo_ret.r), 128, w_out[512 + h * 128:512 + (h + 1) * 128, :], D,
                        gcr.ap[:, h:h + 1], gcr.r, piece=1024)
        maskb = AB.alloc("maskb", 896)
        dma("pool", maskb.ap, cdr["c_mask"], [], [maskb.r], maskb.r)
        kblk = [AB.alloc("kblk0", 512), AB.alloc("kblk1", 512)]
        QTs = [AB.alloc("QTs0", 4 * TB), AB.alloc("QTs1", 4 * TB)]
        mrTs2 = [AB.alloc("mrTs0", 4 * TB), AB.alloc("mrTs1", 4 * TB)]
        msbT = AB.alloc("msbT", 8 * TB, parts=64)
        NSL = 4
        NEL = 6
        e_sb = [AFa.alloc(f"e_sb{i}", TB) for i in range(NEL)]
        g_sb = [AFa.alloc(f"g_sb{i}", TB) for i in range(3)]
        sp_sb = [AB.alloc(f"sp_sb{i}", TB) for i in range(NSL)]
        srun = [AB.alloc(f"srun{i}", TB) for i in range(2)]
        a_sb = [AB.alloc(f"a_sb{i}", TB) for i in range(3)]
        OTs = [AFa.alloc(f"OTs{i}", TB, parts=64) for i in range(2)]
        sq_sb = AB.alloc("sq_sb", TB, parts=64)
        rt_sb = AFa.alloc("rt_sb", TB, parts=64)
        xt2 = [AFa.alloc("xt2_0", D), AFa.alloc("xt2_1", D)]
        h2s = [AFa.alloc("h2s_0", D), AFa.alloc("h2s_1", D)]
        ZB = [pbank[0], pbank[1]]
        CB = [pbank[2], pbank[3]]
        OB = [pbank[4], pbank[5]]
        MSB = pbank[6]
        KT3 = v3(KT.ap, 4)
        Vt3 = v3(Vt.ap, 34)
        ucnt = [0]

        def a2_seq(kind, si, sidx):
            blocks = []
            if kind == "p":
                blocks.append((0, NMETA))
                for c in range(SEQ // 128):
                    blocks.append((NMETA + 128 * c, 128))
                ksrc = lambda n0, nk: sbk_p[si, :, n0:n0 + nk, :].rearrange("h n d -> n h d")
                vsrc = lambda n0, nk: sbv_p[si, :, n0:n0 + nk, :].rearrange("h n d -> n h d")
            else:
                for c in range(PAST // 128):
                    blocks.append((128 * c, 128))
                blocks.append((PAST, DEC))

                def ksrc(n0, nk):
                    if n0 < PAST:
                        return ck[:, n0:n0 + nk, :].rearrange("h n d -> n h d")
                    return sbk_s[:, n0 - PAST:n0 - PAST + nk, :].rearrange("h n d -> n h d")

                def vsrc(n0, nk):
                    if n0 < PAST:
                        return cv[:, n0:n0 + nk, :].rearrange("h n d -> n h d")
                    return sbv_s[:, n0 - PAST:n0 - PAST + nk, :].rearrange("h n d -> n h d")
            for bi, (n0, nk) in enumerate(blocks):
                kb = kblk[bi % 2]
                dma("pool", kb.ap[0:nk, :].rearrange("p (h d) -> p h d", h=8), ksrc(n0, nk), [], [kb.r], kb.r)
                tp3 = transposes_to(None, kb, nk, 4)
                cpy("act" if bi % 2 else "dve", KT3[:, :, n0:n0 + nk], tp3[:, :, 0:nk], [TP.r], [KT.r])
                dma("pool", Vt3[0:nk, bi, :].rearrange("p (h d) -> p h d", h=8), vsrc(n0, nk), [], [Vt.r], Vt.r)

            tiles = []
            if kind == "p":
                tiles.append((0, NMETA, [(0, 0)]))
                for i in range(SEQ // TB):
                    nb = TB // 128
                    kl = [(1 + nb * i + r, r) for r in range(nb - 1, -1, -1)]
                    kl += [(b, None) for b in range(nb * i, 0, -1)]
                    kl += [(0, None)]
                    tiles.append((NMETA + TB * i, TB, kl))
            else:
                kl = [(PAST // 128, 0)] + [(b, None) for b in range(PAST // 128 - 1, -1, -1)]
                tiles.append((0, DEC, kl))

            units = []
            for ti, (qn0, T, kl) in enumerate(tiles):
                for h in range(8):
                    for bi, (blk, mr) in enumerate(kl):
                        units.append((ti, qn0, T, h, bi, blk, mr, len(kl)))
            msb3 = v3(msbT.ap, 8)
            ubase = ucnt[0]
            ucnt[0] += len(units)

            def clo(mr, T):
                return 128 * mr if (mr is not None and T == TB) else 0

            def st0(ui):
                ti, qn0, T, h, bi, blk, mr, nblk = units[ui]
                u = ubase + ui
                qs = QTs[ti % 2]
                qs3 = v3(qs.ap, 4)
                if h == 0 and bi == 0:
                    goff = SEQ_OFF[sidx] + qn0
                    mr_ = mrTs2[ti % 2]
                    dma("sp", qs3[:, :, 0:T], qt_scr[:, :, goff:goff + T].rearrange("a p n -> p a n"), [], [qs.r], qs.r)
                    dma("sp", v3(mr_.ap, 4)[:, :, 0:T], mrt_scr[:, :, goff:goff + T].rearrange("a p n -> p a n"),
                        [], [mr_.r], mr_.r)
                p, jj = h // 2, h % 2
                hp0 = jj * 64
                n0, nk = blocks[blk]
                zb = ZB[u % 2]
                cl = clo(mr, T)
                mm(zb.ap[0:nk, cl:T], KT3[hp0:hp0 + 64, p, n0:n0 + nk], qs3[hp0:hp0 + 64, p, cl:T], True, True,
                   [KT.r, qs.r], [zb.r], grp=cur_grp[0])

            def st1(ui):
                ti, qn0, T, h, bi, blk, mr, nblk = units[ui]
                u = ubase + ui
                n0, nk = blocks[blk]
                zb = ZB[u % 2]
                es_ = e_sb[u % NEL]
                cl = clo(mr, T)
                act(es_.ap[0:nk, cl:T], zb.ap[0:nk, cl:T], AF.Exp, [zb.r], [es_.r], scale=0.125)
                if mr is not None:
                    c0 = 384 - 128 * mr
                    tt("dve", es_.ap[0:nk, cl:T], es_.ap[0:nk, cl:T], maskb.ap[0:nk, c0 + cl:c0 + T], ALU.mult,
                       [es_.r, maskb.r], [es_.r])

            def st1b(ui):
                ti, qn0, T, h, bi, blk, mr, nblk = units[ui]
                u = ubase + ui
                n0, nk = blocks[blk]
                es_ = e_sb[u % NEL]
                sps = sp_sb[u % NSL]
                cl = clo(mr, T)
                act(sps.ap[0:nk, cl:T], es_.ap[0:nk, cl:T], AF.Ln, [es_.r], [sps.r], bias=1.0)

            def st2(ui):
                ti, qn0, T, h, bi, blk, mr, nblk = units[ui]
                u = ubase + ui
                n0, nk = blocks[blk]
                cb = CB[u % 2]
                sps = sp_sb[u % NSL]
                cl = clo(mr, T)
                mm(cb.ap[0:nk, cl:T], tri.ap[0:nk, 0:nk], sps.ap[0:nk, cl:T], True, bi == 0, [tri.r, sps.r], [cb.r],
                   grp=cur_grp[0])
                if bi > 0:
                    sr = srun[(bi - 1) % 2]
                    mm(cb.ap[0:nk, cl:T], ones.ap[0:128, 0:nk], sr.ap[0:128, cl:T], False, True, [ones.r, sr.r], [cb.r],
                       grp=cur_grp[0])
                if bi + 1 < nblk:
                    if bi == 0:
                        mset("pool", srun[0].ap[:, 0:T], 0.0, [srun[0].r])
                        if cl > 0:
                            mset("pool", srun[1].ap[:, 0:T], 0.0, [srun[1].r])
                        cpy("pool", srun[0].ap[0:nk, cl:T], sps.ap[0:nk, cl:T], [sps.r], [srun[0].r])
                    else:
                        so, sn_ = srun[(bi - 1) % 2], srun[bi % 2]
                        assert nk == 128
                        tt("pool", sn_.ap[:, cl:T], so.ap[:, cl:T], sps.ap[0:nk, cl:T], ALU.add, [so.r, sps.r], [sn_.r])

            def st3(ui):
                ti, qn0, T, h, bi, blk, mr, nblk = units[ui]
                u = ubase + ui
                n0, nk = blocks[blk]
                cb = CB[u % 2]
                gs = g_sb[u % 3]
                cl = clo(mr, T)
                act(gs.ap[0:nk, cl:T], cb.ap[0:nk, cl:T], AF.Exp, [cb.r], [gs.r], scale=-1.0)

            def st3b(ui):
                ti, qn0, T, h, bi, blk, mr, nblk = units[ui]
                u = ubase + ui
                n0, nk = blocks[blk]
                es_ = e_sb[u % NEL]
                gs = g_sb[u % 3]
                as_ = a_sb[u % 3]
                cl = clo(mr, T)
                if cl > 0 and bi == 0:
                    mset("pool", as_.ap[0:nk, 0:cl], 0.0, [as_.r])
                tt("dve", as_.ap[0:nk, cl:T], es_.ap[0:nk, cl:T], gs.ap[0:nk, cl:T], ALU.mult, [es_.r, gs.r], [as_.r])

            def st4(ui):
                ti, qn0, T, h, bi, blk, mr, nblk = units[ui]
                u = ubase + ui
                n0, nk = blocks[blk]
                as_ = a_sb[u % 3]
                ob = OB[h % 2]
                cl = 0 if bi == 0 else clo(mr, T)
                mm(ob.ap[0:64, cl:T], Vt3[0:nk, blk, h * 64:(h + 1) * 64], as_.ap[0:nk, cl:T],
                   bi == 0, bi == nblk - 1, [Vt.r, as_.r], [ob.r], grp=cur_grp[0])
                if bi != nblk - 1:
                    return
                ot = OTs[h % 2]
                cpy("act", ot.ap[:, 0:T], ob.ap[0:64, 0:T], [ob.r], [ot.r])
                act(sq_sb.ap[:, 0:T], ob.ap[0:64, 0:T], AF.Square, [ob.r], [sq_sb.r])
                mm(MSB.ap[0:64, 0:T], od64.ap[0:64, 0:64], sq_sb.ap[:, 0:T], True, True, [od64.r, sq_sb.r], [MSB.r])
                act(rt_sb.ap[:, 0:T], MSB.ap[0:64, 0:T], AF.Sqrt, [MSB.r], [rt_sb.r], bias=EPS)
                recip(rt_sb.ap[:, 0:T], rt_sb.ap[:, 0:T], [rt_sb.r], [rt_sb.r])
                tt("dve", msb3[:, h, 0:T], ot.ap[:, 0:T], rt_sb.ap[:, 0:T], ALU.mult, [ot.r, rt_sb.r], [msbT.r])
                if h != 7:
                    return
                goff = SEQ_OFF[sidx] + qn0
                mr_ = mrTs2[ti % 2]
                nsub = (T + 127) // 128
                for sb_ in range(nsub):
                    t0 = sb_ * 128
                    ts_ = min(128, T - t0)
                    xx = xt2[sb_ % 2]
                    hh = h2s[sb_ % 2]
                    dma("sp", xx.ap[0:ts_, :], x_rows(kind, si, qn0 + t0, ts_), [], [xx.r], xx.r)
                    for cg in range(2):
                        b = MSB if cg == 0 else OB[1]
                        for hd in range(8):
                            mm(b.ap[0:ts_, :], msb3[:, hd, t0:t0 + ts_], wos3[:, hd, cg * 512:(cg + 1) * 512],
                               hd == 0, False, [msbT.r, wo_sb.r], [b.r])
                        for hd in range(4):
                            mm(b.ap[0:ts_, :], v3(mr_.ap, 4)[:, hd, t0:t0 + ts_], wor3[:, hd, cg * 512:(cg + 1) * 512],
                               False, hd == 3, [mr_.r, wo_ret.r], [b.r])
                        tt("dve", hh.ap[0:ts_, cg * 512:(cg + 1) * 512], xx.ap[0:ts_, cg * 512:(cg + 1) * 512],
                           b.ap[0:ts_, :], ALU.add, [xx.r, b.r], [hh.r])
                    dma("sp", h2_scr[goff + t0:goff + t0 + ts_, :], hh.ap[0:ts_, :], [hh.r], [], hh.r)

            stages = (st0, st1, st1b, st2, st3, st3b, st4)
            cur_grp = [None]
            for it in range(len(units) + len(stages) - 1):
                cur_grp[0] = None
                for sidx_, stf in enumerate(stages):
                    ui = it - sidx_
                    if 0 <= ui < len(units):
                        stf(ui)

        for sidx, (kind, si) in enumerate(seqs if STAGE >= 4 else []):
            a2_seq(kind, si, sidx)

        S.barrier()
        AB.off = AB_BASE
        w_up_bf = AB.alloc("w_up_bf", 8 * 2 * DFF)
        w_dn_bf = AB.alloc("w_dn_bf", 22 * D)
        gcf = AFa.alloc("gcf", 8)
        cwt = AFa.alloc("cwt", NFC * 4)
        cw3 = v3(cwt.ap, NFC)
        gfin = AFa.alloc("gfin", D)
        carry = AFa.alloc("carry", NFC * 2)
        car3 = v3(carry.ap, NFC)
        xb = [AFa.alloc("xb0", D), AFa.alloc("xb1", D)]
        junkb = AFa.alloc("junkb", D)
        statb = AFa.alloc("statb", 16)
        hnb = AB.alloc("hnb", D)
        hT2 = AB.alloc("hT2", 8 * TB)
        gvT = AB.alloc("gvT", 22 * TB)
        cgate = [AFa.alloc("cgate0", TB), AFa.alloc("cgate1", TB)]
        B_MARK = AB.off
        wstage = [AFa.alloc("wst0c", 1408), AFa.alloc("wst1c", 1408)]
        dma("sp", gcf.ap, g_ffn.rearrange("(c p) -> p c", p=128), [], [gcf.r], gcf.r, slow=True)
        wu3 = v3(w_up_bf.ap, 8)
        wd3 = v3(w_dn_bf.ap, 22)
        for k in range(8):
            load_weight(Buf(wu3[:, k, :], w_up_bf.r), 128, w_up[k * 128:(k + 1) * 128, :], 2 * DFF,
                        gcf.ap[:, k:k + 1], gcf.r, piece=1408)
        for k in range(22):
            load_weight(Buf(wd3[:, k, :], w_dn_bf.r), 128, w_down[k * 128:(k + 1) * 128, :], D, None, None, piece=1024)
        for i in range(3):
            dma("sp", cw3[:, :, i], conv_w[i].rearrange("(c p) -> p c", p=128), [], [cwt.r], cwt.r, slow=True)
        dma("sp", cw3[:, :, 3], conv_b.rearrange("(c p) -> p c", p=128), [], [cwt.r], cwt.r, slow=True)
        dma("sp", gfin.ap, g_fin.partition_broadcast(128), [], [gfin.r], gfin.r)
        S.barrier()
        AB.off = B_MARK
        NUE = 4
        uext = [AFa.alloc(f"uext{i}", TB + 2) for i in range(NUE)]
        t1b = [AFa.alloc(f"t1b{i}", TB) for i in range(NUE)]
        UB = [pbank[0], pbank[1], pbank[2]]
        FB = [pbank[3], pbank[4]]
        hT23 = v3(hT2.ap, 8)
        gv3 = v3(gvT.ap, 22)
        ub_i = [0]
        deferred_b = []

        xpb = AFa.alloc("xpb", D)
        btiles = []
        for sidx, (kind, si) in enumerate(seqs if STAGE >= 6 else []):
            tl = ([(0, NMETA)] + [(NMETA + TB * i, TB) for i in range(SEQ // TB)]) if kind == "p" else [(0, DEC)]
            for ti, (n0, T) in enumerate(tl):
                btiles.append((kind, si, sidx, n0, T, ti == 0, ti == len(tl) - 1))

        def b_nsub(tile):
            return (tile[4] + 127) // 128

        def b_prep_rms(tile, sb_):
            kind, si, sidx, n0, T, fst, lst = tile
            goff = SEQ_OFF[sidx] + n0
            t0 = sb_ * 128
            ts_ = min(128, T - t0)
            dma("sp", xpb.ap[0:ts_, :], h2_scr[goff + t0:goff + t0 + ts_, :], [], [xpb.r], xpb.r)
            rms_rstd(xpb, ts_, junkb, statb)
            ts1("dve", hnb.ap[0:ts_, :], xpb.ap[0:ts_, :], statb.ap[0:ts_, 1:2], None, ALU.mult, None,
                [xpb.r, statb.r], [hnb.r])

        def b_prep_tr(tile, sb_):
            kind, si, sidx, n0, T, fst, lst = tile
            t0 = sb_ * 128
            ts_ = min(128, T - t0)
            tp3 = transposes_to(None, hnb, ts_, 8)
            cpy("act", hT23[:, :, t0:t0 + ts_], tp3[:, :, 0:ts_], [TP.r], [hT2.r])

        def b_up(tile):
            kind, si, sidx, n0, T, fst, lst = tile
            if fst:
                if kind == "p":
                    mset("pool", carry.ap, 0.0, [carry.r])
                else:
                    for r_ in range(2):
                        dma("sp", car3[:, :, r_], sconv[r_].rearrange("(c p) -> p c", p=128), [], [carry.r], carry.r,
                            slow=True)
            for fc in range(22):
                for which in range(2):
                    f = fc + 22 * which
                    ub = UB[ub_i[0] % 3]
                    ue = uext[ub_i[0] % NUE]
                    t1_ = t1b[ub_i[0] % NUE]
                    cgt = cgate[fc % 2]
                    ub_i[0] += 1
                    for k in range(8):
                        mm(ub.ap[:, 0:T], wu3[:, k, f * 128:(f + 1) * 128], hT23[:, k, 0:T], k == 0, k == 7,
                           [w_up_bf.r, hT2.r], [ub.r])
                    cpy("act", ue.ap[:, 0:2], car3[:, f, :], [carry.r], [ue.r])
                    cpy("act", ue.ap[:, 2:T + 2], ub.ap[:, 0:T], [ub.r], [ue.r])
                    cpy("act", car3[:, f, :], ue.ap[:, T:T + 2], [ue.r], [carry.r])
                    ts1("pool", t1_.ap[:, 0:T], ue.ap[:, 0:T], cw3[:, f, 0:1], cw3[:, f, 3:4], ALU.mult, ALU.add,
                        [ue.r, cwt.r], [t1_.r])
                    stt("dve", t1_.ap[:, 0:T], ue.ap[:, 1:T + 1], cw3[:, f, 1:2], t1_.ap[:, 0:T], ALU.mult, ALU.add,
                        [ue.r, cwt.r, t1_.r], [t1_.r])

                    def mk_c(ue=ue, t1_=t1_, f=f, T=T):
                        stt("dve", t1_.ap[:, 0:T], ue.ap[:, 2:T + 2], cw3[:, f, 2:3], t1_.ap[:, 0:T],
                            ALU.mult, ALU.add, [ue.r, cwt.r, t1_.r], [t1_.r])

                    def mk_silu(cgt=cgt, t1_=t1_, T=T):
                        act(cgt.ap[:, 0:T], t1_.ap[:, 0:T], AF.Silu, [t1_.r], [cgt.r])

                    def mk_mult(cgt=cgt, t1_=t1_, fc=fc, T=T):
                        tt("dve", gv3[:, fc, 0:T], t1_.ap[:, 0:T], cgt.ap[:, 0:T], ALU.mult, [t1_.r, cgt.r], [gvT.r])

                    deferred_b.append([1, mk_c])
                    deferred_b.append([2, mk_silu if which == 0 else mk_mult])
                    for ent in list(deferred_b):
                        if ent[0] == 0:
                            ent[1]()
                            deferred_b.remove(ent)
                        else:
                            ent[0] -= 1
            while deferred_b:
                for ent in list(deferred_b):
                    if ent[0] == 0:
                        ent[1]()
                        deferred_b.remove(ent)
                    else:
                        ent[0] -= 1

        def b_down(tile, sb_):
            kind, si, sidx, n0, T, fst, lst = tile
            goff = SEQ_OFF[sidx] + n0
            t0 = sb_ * 128
            ts_ = min(128, T - t0)
            xx = xb[sb_ % 2]
            dma("sp", xx.ap[0:ts_, :], h2_scr[goff + t0:goff + t0 + ts_, :], [], [xx.r], xx.r)
            for cg in range(2):
                b = FB[cg]
                for kc in range(22):
                    mm(b.ap[0:ts_, :], gv3[:, kc, t0:t0 + ts_], wd3[:, kc, cg * 512:(cg + 1) * 512],
                       kc == 0, kc == 21, [gvT.r, w_dn_bf.r], [b.r])
                tt("dve", xx.ap[0:ts_, cg * 512:(cg + 1) * 512], xx.ap[0:ts_, cg * 512:(cg + 1) * 512],
                   b.ap[0:ts_, :], ALU.add, [xx.r, b.r], [xx.r])
            rms_rstd(xx, ts_, junkb, statb)
            stt("dve", xx.ap[0:ts_, :], xx.ap[0:ts_, :], statb.ap[0:ts_, 1:2], gfin.ap[0:ts_, :],
                ALU.mult, ALU.mult, [xx.r, statb.r, gfin.r], [xx.r])
            if kind == "p":
                f0 = n0 - NMETA + t0
                dst = y_p[si, f0:f0 + ts_, :]
            else:
                dst = y_s[n0 + t0:n0 + t0 + ts_, :]
            dma("sp", dst, xx.ap[0:ts_, :], [xx.r], [], xx.r)

        def b_fin(tile):
            kind, si, sidx, n0, T, fst, lst = tile
            if not lst:
                return
            for r_ in range(2):
                if kind == "p":
                    dst = conv_p[si, r_].rearrange("(c p) -> p c", p=128)
                else:
                    dst = conv_s[r_].rearrange("(c p) -> p c", p=128)
                dma("sp", dst, car3[:, :, r_], [carry.r], [], carry.r, slow=True)

        if btiles:
            for k in range(b_nsub(btiles[0])):
                b_prep_rms(btiles[0], k)
                b_prep_tr(btiles[0], k)
        for i_, tile in enumerate(btiles):
            b_up(tile)
            nxt = btiles[i_ + 1] if i_ + 1 < len(btiles) else None
            is_meta = tile[0] == "p" and tile[3] == 0
            ns_cur = 0 if is_meta else b_nsub(tile)
            ns_nxt = b_nsub(nxt) if nxt is not None else 0
            for k in range(max(ns_cur, ns_nxt)):
                if k < ns_nxt:
                    b_prep_rms(nxt, k)
                if k < ns_cur:
                    b_down(tile, k)
                if k < ns_nxt:
                    b_prep_tr(nxt, k)
            b_fin(tile)
        out_res.extend([xb[0].r, xb[1].r, carry.r])
        S.emit(final_wait_res=out_res)
    return nc, consts


_CACHE = {}


def kernel(x_prompt, x_sample, cache_sb_k, cache_sb_v, state_ret, state_conv, meta_tokens, g_norm_mix, w_in,
           g_sb_out, g_ret_out, w_out, g_norm_ffn, w_up, conv_w, conv_b, w_down, g_norm_final):
    if "nc" not in _CACHE:
        _CACHE["nc"] = build_program()
    nc, consts = _CACHE["nc"]
    f = lambda a: np.ascontiguousarray(np.asarray(a, dtype=np.float32))
    x_prompt, x_sample = f(x_prompt), f(x_sample)
    cache_sb_k, cache_sb_v = f(cache_sb_k), f(cache_sb_v)
    state_ret, state_conv = f(state_ret), f(state_conv)
    shared = {
        "meta": f(meta_tokens), "g_mix": f(g_norm_mix), "w_in": f(w_in), "g_sbo": f(g_sb_out),
        "g_reto": f(g_ret_out), "w_out": f(w_out), "g_ffn": f(g_norm_ffn), "w_up": f(w_up),
        "conv_w": f(conv_w), "conv_b": f(conv_b), "w_down": f(w_down), "g_fin": f(g_norm_final),
    }
    shared.update(consts)
    in_maps = []
    for c in range(NCORES):
        m = dict(shared)
        m["xp"] = x_prompt[2 * c:2 * c + 2]
        m["xs"] = x_sample[c]
        m["ck"] = cache_sb_k[c]
        m["cv"] = cache_sb_v[c]
        m["sret"] = state_ret[c]
        m["sconv"] = state_conv[c]
        in_maps.append(m)
    res = run_bass_kernel_spmd(nc, in_maps, core_ids=list(range(NCORES)))
    rs = res.results
    cat = lambda k: np.concatenate([np.asarray(r[k], dtype=np.float32) for r in rs], axis=0)
    stk = lambda k: np.stack([np.asarray(r[k], dtype=np.float32) for r in rs], axis=0)
    return (cat("y_p"), stk("y_s"), cat("sbk_p"), cat("sbv_p"), cat("ret_p"), cat("conv_p"),
            stk("sbk_s"), stk("sbv_s"), stk("ret_s"), stk("conv_s"))
```
